# Optimizing a Trainium2 kernel written in Bass

```python
import jax, jax.numpy as jnp
from jax import lax
import numpy as np

D_MODEL = 1024
BATCH = 8
SEQ = 2048
DEPTH = 4
DEC_BATCH = 128
DEC_SEQ = 8
PAST_LEN = 16384
PAGE_SIZE = 128

N_MEM = 256
EXPAND = 2
BRANCH_WIDTH = EXPAND * D_MODEL
XA_HEADS = 4
XA_HEAD_DIM = D_MODEL // 8
XA_WIDTH = XA_HEADS * XA_HEAD_DIM
MIX_WIDTH = BRANCH_WIDTH - XA_WIDTH
SHORT_CONV_W = 3
CONFORMER_CONV_W = 31
N_A = (DEPTH + 1) // 2
N_B = DEPTH // 2
A_IN_WIDTH = 3 * MIX_WIDTH + BRANCH_WIDTH + XA_WIDTH
B_IN_WIDTH = 2 * MIX_WIDTH + BRANCH_WIDTH + XA_WIDTH
EPS = 1e-6

kernel_name = "hybrid_conv_memory_decoder_step"


def rms_norm(x, g):
    xf = x.astype(jnp.float32)
    y = xf * lax.rsqrt(jnp.mean(xf * xf, axis=-1, keepdims=True) + EPS)
    return (y * g.astype(jnp.float32)).astype(x.dtype)


def layer_norm(x, g, b):
    xf = x.astype(jnp.float32)
    mu = jnp.mean(xf, axis=-1, keepdims=True)
    xc = xf - mu
    var = jnp.mean(xc * xc, axis=-1, keepdims=True)
    y = xc * lax.rsqrt(var + EPS) * g.astype(jnp.float32) + b.astype(jnp.float32)
    return y.astype(x.dtype)


def causal_dwconv(u, state, w):
    full = jnp.concatenate([state.astype(u.dtype), u], axis=1)
    y = lax.conv_general_dilated(
        full, w[:, None, :].astype(u.dtype), window_strides=(1,), padding="VALID",
        dimension_numbers=("NWC", "WIO", "NWC"), feature_group_count=u.shape[-1])
    return y, full[:, -(w.shape[0] - 1):, :]


def memory_kv(mem, g, w_kv):
    b, m, _ = mem.shape
    kv = jnp.einsum("bmd,de->bme", rms_norm(mem, g), w_kv)
    k, v = jnp.split(kv, 2, axis=-1)
    return (k.reshape(b, m, XA_HEADS, XA_HEAD_DIM), v.reshape(b, m, XA_HEADS, XA_HEAD_DIM))


def memory_attention(q, k, v):
    b, t, _ = q.shape
    qh = q.reshape(b, t, XA_HEADS, XA_HEAD_DIM)
    s = jnp.einsum("bthd,bmhd->bhtm", qh, k).astype(jnp.float32) * (XA_HEAD_DIM ** -0.5)
    p = jax.nn.softmax(s, axis=-1).astype(v.dtype)
    o = jnp.einsum("bhtm,bmhd->bthd", p, v)
    return o.reshape(b, t, XA_WIDTH)


def short_conv_layer(x, conv_state, mem_k, mem_v, g_pre, g_post, w_in, conv_w, w_out):
    proj = jnp.einsum("btd,de->bte", rms_norm(x, g_pre), w_in)
    hin, b_gate, c_gate, z, q = jnp.split(
        proj, [MIX_WIDTH, 2 * MIX_WIDTH, 3 * MIX_WIDTH, 3 * MIX_WIDTH + BRANCH_WIDTH], axis=-1)
    conv, new_state = causal_dwconv(c_gate * hin, conv_state, conv_w)
    mix = b_gate * conv
    xa = memory_attention(q, mem_k, mem_v)
    branch = jnp.concatenate([mix, xa], axis=-1) * jax.nn.silu(z)
    out = jnp.einsum("bte,ed->btd", branch, w_out)
    return x + rms_norm(out, g_post), new_state


def conformer_layer(x, conv_state, mem_k, mem_v, g_pre, g_post, w_in, conv_w, conv_b,
                    ln_g, ln_b, w_out):
    proj = jnp.einsum("btd,de->bte", rms_norm(x, g_pre), w_in)
    val, glu, z, q = jnp.split(
        proj, [MIX_WIDTH, 2 * MIX_WIDTH, 2 * MIX_WIDTH + BRANCH_WIDTH], axis=-1)
    u = val * jax.nn.sigmoid(glu)
    conv, new_state = causal_dwconv(u, conv_state, conv_w)
    mix = jax.nn.silu(layer_norm(conv + conv_b, ln_g, ln_b))
    xa = memory_attention(q, mem_k, mem_v)
    branch = jnp.concatenate([mix, xa], axis=-1) * jax.nn.silu(z)
    out = jnp.einsum("bte,ed->btd", branch, w_out)
    return x + rms_norm(out, g_post), new_state


def setup_inputs(seed: int = 0) -> dict:
    key = jax.random.key(seed)
    ks = iter(jax.random.split(key, 40))
    f32 = jnp.float32

    def nrm(shape, scale=1.0):
        return jax.random.normal(next(ks), shape, f32) * scale

    def gain(shape):
        return 1.0 + nrm(shape, 0.05)

    return {
        "x_prompt": nrm((BATCH, SEQ, D_MODEL)),
        "x_sample": nrm((DEC_BATCH, DEC_SEQ, D_MODEL)),
        "mem_prompt": nrm((BATCH, N_MEM, D_MODEL)),
        "cache_mem_k": nrm((DEPTH, DEC_BATCH, N_MEM, XA_HEADS, XA_HEAD_DIM)),
        "cache_mem_v": nrm((DEPTH, DEC_BATCH, N_MEM, XA_HEADS, XA_HEAD_DIM)),
        "state_conv_a": nrm((N_A, DEC_BATCH, SHORT_CONV_W - 1, MIX_WIDTH)),
        "state_conv_b": nrm((N_B, DEC_BATCH, CONFORMER_CONV_W - 1, MIX_WIDTH)),
        "a_norm_pre": gain((N_A, D_MODEL)),
        "a_norm_post": gain((N_A, D_MODEL)),
        "a_mem_norm": gain((N_A, D_MODEL)),
        "a_w_in": nrm((N_A, D_MODEL, A_IN_WIDTH), D_MODEL ** -0.5),
        "a_conv_w": nrm((N_A, SHORT_CONV_W, MIX_WIDTH), SHORT_CONV_W ** -0.5),
        "a_w_kv": nrm((N_A, D_MODEL, 2 * XA_WIDTH), D_MODEL ** -0.5),
        "a_w_out": nrm((N_A, BRANCH_WIDTH, D_MODEL), BRANCH_WIDTH ** -0.5),
        "b_norm_pre": gain((N_B, D_MODEL)),
        "b_norm_post": gain((N_B, D_MODEL)),
        "b_mem_norm": gain((N_B, D_MODEL)),
        "b_w_in": nrm((N_B, D_MODEL, B_IN_WIDTH), D_MODEL ** -0.5),
        "b_conv_w": nrm((N_B, CONFORMER_CONV_W, MIX_WIDTH), CONFORMER_CONV_W ** -0.5),
        "b_conv_b": nrm((N_B, MIX_WIDTH), 0.02),
        "b_ln_g": gain((N_B, MIX_WIDTH)),
        "b_ln_b": nrm((N_B, MIX_WIDTH), 0.02),
        "b_w_kv": nrm((N_B, D_MODEL, 2 * XA_WIDTH), D_MODEL ** -0.5),
        "b_w_out": nrm((N_B, BRANCH_WIDTH, D_MODEL), BRANCH_WIDTH ** -0.5),
    }


def reference(x_prompt, x_sample, mem_prompt, cache_mem_k, cache_mem_v, state_conv_a,
              state_conv_b, a_norm_pre, a_norm_post, a_mem_norm, a_w_in, a_conv_w, a_w_kv,
              a_w_out, b_norm_pre, b_norm_post, b_mem_norm, b_w_in, b_conv_w, b_conv_b,
              b_ln_g, b_ln_b, b_w_kv, b_w_out):
    y_p, y_s = x_prompt, x_sample
    n_prompt = x_prompt.shape[0]
    mk_p, mv_p, ca_p, cb_p, ca_s, cb_s = [], [], [], [], [], []
    for i in range(DEPTH):
        j = i // 2
        if i % 2 == 0:
            k_p, v_p = memory_kv(mem_prompt, a_mem_norm[j], a_w_kv[j])
            zero_state = jnp.zeros((n_prompt, SHORT_CONV_W - 1, MIX_WIDTH), x_prompt.dtype)
            y_p, st_p = short_conv_layer(y_p, zero_state, k_p, v_p, a_norm_pre[j],
                                         a_norm_post[j], a_w_in[j], a_conv_w[j], a_w_out[j])
            y_s, st_s = short_conv_layer(y_s, state_conv_a[j], cache_mem_k[i], cache_mem_v[i],
                                         a_norm_pre[j], a_norm_post[j], a_w_in[j],
                                         a_conv_w[j], a_w_out[j])
            ca_p.append(st_p)
            ca_s.append(st_s)
        else:
            k_p, v_p = memory_kv(mem_prompt, b_mem_norm[j], b_w_kv[j])
            zero_state = jnp.zeros((n_prompt, CONFORMER_CONV_W - 1, MIX_WIDTH), x_prompt.dtype)
            y_p, st_p = conformer_layer(y_p, zero_state, k_p, v_p, b_norm_pre[j],
                                        b_norm_post[j], b_w_in[j], b_conv_w[j], b_conv_b[j],
                                        b_ln_g[j], b_ln_b[j], b_w_out[j])
            y_s, st_s = conformer_layer(y_s, state_conv_b[j], cache_mem_k[i], cache_mem_v[i],
                                        b_norm_pre[j], b_norm_post[j], b_w_in[j], b_conv_w[j],
                                        b_conv_b[j], b_ln_g[j], b_ln_b[j], b_w_out[j])
            cb_p.append(st_p)
            cb_s.append(st_s)
        mk_p.append(k_p)
        mv_p.append(v_p)
    mem_k_prompt = jnp.stack(mk_p)
    mem_v_prompt = jnp.stack(mv_p)
    conv_a_prompt = jnp.stack(ca_p)
    conv_b_prompt = jnp.stack(cb_p)
    conv_a_sample = jnp.stack(ca_s)
    conv_b_sample = jnp.stack(cb_s)
    return (y_p, y_s, mem_k_prompt, mem_v_prompt, conv_a_prompt, conv_b_prompt,
            conv_a_sample, conv_b_sample)
```

```python
from contextlib import ExitStack
import numpy as np
import concourse.bass as bass
import concourse.mybir as mybir
from concourse.bass_utils import run_bass_kernel_spmd

F32 = mybir.dt.float32
BF16 = mybir.dt.bfloat16
AF = mybir.ActivationFunctionType
ALU = mybir.AluOpType

_DT_SIZE = {"float32": 4, "bfloat16": 2, "float32r": 4, "int32": 4, "uint32": 4,
            "float16": 2, "int16": 2, "uint16": 2, "int8": 1, "uint8": 1}


def _esz(dt):
    return _DT_SIZE[str(dt).split(".")[-1]]


class _Rec:
    __slots__ = ("plo", "phi", "blo", "bhi", "writer", "readers", "dma_readers")

    def __init__(self, plo, phi, blo, bhi, writer):
        self.plo, self.phi, self.blo, self.bhi = plo, phi, blo, bhi
        self.writer = writer
        self.readers = {}
        self.dma_readers = []


class _Op:
    __slots__ = ("q", "idx", "fn", "is_dma", "deps", "needs_inc", "cnt", "dsem", "dtarget")

    def __init__(self, q, idx, fn, is_dma):
        self.q, self.idx, self.fn, self.is_dma = q, idx, fn, is_dma
        self.deps = []
        self.needs_inc = False
        self.cnt = 0
        self.dsem = -1
        self.dtarget = 0


class Prog:
    QUEUES = ("pe", "act", "dve", "pool", "sp")
    NDMASEM = 8
    SAME_DIST = 3

    def __init__(self):
        self.ops = {q: [] for q in self.QUEUES}
        self.recs = {}
        self.dma_count = {q: 0 for q in self.QUEUES}
        self.dma_hist = {q: [] for q in self.QUEUES}

    @staticmethod
    def region(ap):
        t = ap.tensor
        if type(t).__name__.startswith("DRam"):
            return None
        dims = ap.ap
        row = dims[0][0]
        off = int(ap.offset)
        if row > 0:
            p0, f0 = off // row, off % row
        else:
            p0, f0 = 0, off
        npart = dims[0][1]
        span = 1
        for s, c in dims[1:]:
            span += (c - 1) * abs(s)
        e = _esz(ap.dtype)
        if type(t).__name__.startswith("PSum"):
            return (t.name, 0, 128, 0, 2048)
        return (t.name, p0, p0 + npart, f0 * e, (f0 + span) * e)

    def _access(self, op, ap, is_write):
        reg = self.region(ap)
        if reg is None:
            return
        name, plo, phi, blo, bhi = reg
        recs = self.recs.setdefault(name, [])
        deps = op.deps
        if is_write:
            keep = []
            for r in recs:
                if r.plo < phi and plo < r.phi and r.blo < bhi and blo < r.bhi:
                    if r.writer is not None:
                        deps.append(r.writer)
                    deps.extend(r.readers.values())
                    deps.extend(r.dma_readers)
                    if plo <= r.plo and r.phi <= phi and blo <= r.blo and r.bhi <= bhi:
                        continue
                keep.append(r)
            keep.append(_Rec(plo, phi, blo, bhi, op))
            self.recs[name] = keep
        else:
            for r in recs:
                if r.plo < phi and plo < r.phi and r.blo < bhi and blo < r.bhi:
                    if r.writer is not None:
                        deps.append(r.writer)
                    if op.is_dma:
                        r.dma_readers.append(op)
                    else:
                        r.readers[op.q] = op

    def op(self, q, fn, reads=(), writes=(), dma=False):
        lst = self.ops[q]
        o = _Op(q, len(lst), fn, dma)
        for ap in reads:
            self._access(o, ap, False)
        for ap in writes:
            self._access(o, ap, True)
        if dma:
            k = self.dma_count[q]
            self.dma_count[q] = k + 1
            o.dsem = k % self.NDMASEM
            o.dtarget = 16 * (k // self.NDMASEM + 1)
            hist = self.dma_hist[q]
            if k >= self.NDMASEM:
                o.deps.append(hist[k - self.NDMASEM])
            hist.append(o)
        nd = []
        seen = set()
        for d in o.deps:
            if d is o or id(d) in seen:
                continue
            seen.add(id(d))
            if (not d.is_dma) and d.q == q and (not dma):
                if q == "pe":
                    continue
                if o.idx - d.idx > self.SAME_DIST:
                    continue
            if not d.is_dma:
                d.needs_inc = True
            nd.append(d)
        o.deps = nd
        lst.append(o)
        return o

    def finish(self):
        o = _Op("sp", len(self.ops["sp"]), None, False)
        for q in self.QUEUES:
            o.deps.extend(self.dma_hist[q][-self.NDMASEM:])
        self.ops["sp"].append(o)

    def emit(self, nc, es):
        for q in self.QUEUES:
            c = 0
            for o in self.ops[q]:
                if o.needs_inc:
                    c += 1
                o.cnt = c
        esem = {q: es.enter_context(nc.semaphore("e_" + q)) for q in self.QUEUES}
        dsem = {q: [es.enter_context(nc.semaphore("d_%s%d" % (q, i))) for i in range(self.NDMASEM)]
                for q in self.QUEUES if self.dma_count[q] > 0}
        block = es.enter_context(nc.Block())

        def mk(q):
            def body(eng):
                waited = {f: 0 for f in self.QUEUES}
                dwaited = {}
                for o in self.ops[q]:
                    need = {}
                    for d in o.deps:
                        if d.is_dma:
                            key = (d.q, d.dsem)
                            if dwaited.get(key, 0) < d.dtarget:
                                dwaited[key] = d.dtarget
                                eng.wait_ge(dsem[d.q][d.dsem], d.dtarget)
                        elif d.cnt > need.get(d.q, 0):
                            need[d.q] = d.cnt
                    for f, c in need.items():
                        if c > waited[f]:
                            waited[f] = c
                            eng.wait_ge(esem[f], c)
                    if o.fn is None:
                        continue
                    ins = o.fn(eng)
                    if o.is_dma:
                        ins.then_inc(dsem[q][o.dsem], 16)
                    elif o.needs_inc:
                        ins.then_inc(esem[q], 1)
            return body

        block.tensor(mk("pe"))
        block.scalar(mk("act"))
        block.vector(mk("dve"))
        block.gpsimd(mk("pool"))
        block.sync(mk("sp"))


D = 1024
KD = 8
MIXW = 1536
NJ = 12
BRW = 2048
XAW = 512
NH = 4
A_IN = 3 * MIXW + BRW + XAW
B_IN = 2 * MIXW + BRW + XAW
NM = 256
SB = 16
ST = 8
SC = SB * ST
EPS = 1e-6
WA = 3
WB = 31

V_A_PRE, V_A_POST, V_A_MEM = 0, 16, 32
V_B_PRE, V_B_POST, V_B_MEM = 48, 64, 80
V_A_CW = 96
V_B_CW = V_A_CW + 2 * NJ * WA
V_B_CB = V_B_CW + 2 * NJ * WB
V_B_LG = V_B_CB + 2 * NJ
V_B_LB = V_B_LG + 2 * NJ
NVEC = V_B_LB + 2 * NJ


def build_program(SEQ=2048, DEPTH=4, NPASS=2, NR=12, TPE_B=27, SRING3=True, HBB=2, HEADPOS=0, POOLPOST=True, DEFER=True, PREMEM=True, DIAGQ="dve"):
    PC = SEQ // NPASS
    TN = PC + SC
    PT = min(512, PC)
    NPT = PC // PT
    HPB = WB - 1
    UBW = HPB + PC + SB * (HPB + ST)
    NA = (DEPTH + 1) // 2
    NB = DEPTH // 2

    nc = bass.Bass("TRN2", target_bir_lowering=False)
    dram = lambda n, s, k: nc.dram_tensor(n, s, F32, kind=k).ap()
    xp_d = dram("xp", [SEQ, D], "ExternalInput")
    xs_d = dram("xs", [SC, D], "ExternalInput")
    mem_d = dram("mem", [NM, D], "ExternalInput")
    ck_d = dram("ck", [DEPTH, SB, NM, XAW], "ExternalInput")
    cv_d = dram("cv", [DEPTH, SB, NM, XAW], "ExternalInput")
    sa_d = dram("sa", [NA, SB * (WA - 1), MIXW], "ExternalInput")
    sb_d = dram("sb", [max(NB, 1), SB * HPB, MIXW], "ExternalInput")
    vec_d = dram("vecs", [128, NVEC], "ExternalInput")
    awin_d = dram("a_w_in", [NA, D, A_IN], "ExternalInput")
    awkv_d = dram("a_w_kv", [NA, D, 2 * XAW], "ExternalInput")
    awout_d = dram("a_w_out", [NA, BRW, D], "ExternalInput")
    bwin_d = dram("b_w_in", [max(NB, 1), D, B_IN], "ExternalInput")
    bwkv_d = dram("b_w_kv", [max(NB, 1), D, 2 * XAW], "ExternalInput")
    bwout_d = dram("b_w_out", [max(NB, 1), BRW, D], "ExternalInput")
    yp_d = dram("yp", [SEQ, D], "ExternalOutput")
    ys_d = dram("ys", [SC, D], "ExternalOutput")
    mk_d = dram("mk", [DEPTH, NM, XAW], "ExternalOutput")
    mv_d = dram("mv", [DEPTH, NM, XAW], "ExternalOutput")
    cap_d = dram("cap", [NA, WA - 1, MIXW], "ExternalOutput")
    cbp_d = dram("cbp", [max(NB, 1), HPB, MIXW], "ExternalOutput")
    cas_d = dram("cas", [NA, SB * (WA - 1), MIXW], "ExternalOutput")
    cbs_d = dram("cbs", [max(NB, 1), SB * HPB, MIXW], "ExternalOutput")

    P = Prog()
    with ExitStack() as es:
        sbt = lambda n, s, d: es.enter_context(nc.sbuf_tensor(n, s, d))
        x = sbt("x", [128, KD, TN], F32)
        xn = sbt("xn", [128, KD, TN], BF16)
        br = sbt("br", [128, 16, TN], BF16)
        SCRW = max(KD * TN, 2 * UBW + TN + max(2 * TN, 2 * TPE_B * 64) + 2 * 512, 2 * D, 8448)
        scr = sbt("scr", [128, SCRW], F32)
        ring = sbt("ring", [128, NR, KD, 128], BF16)
        szt = [sbt("sz%d" % i, [128, 512], F32) for i in range(2)]
        sqt = [sbt("sq%d" % i, [128, 512], BF16) for i in range(2)]
        sq3 = sqt + [sbt("sq2", [128, 512], BF16)]
        szsA = sbt("szsA", [128, SC], F32)
        rstdt = [sbt("rstd%d" % i, [128, 512], F32) for i in range(2)]
        qTt = [sbt("qT%d" % i, [128, 512], BF16) for i in range(2)]
        pTt = [sbt("pT%d" % i, [128, 2, 512], BF16) for i in range(2)]
        att = [sbt("at%d" % i, [128, 512], F32) for i in range(2)]
        memhT = sbt("memhT", [128, KD, NM], F32)
        memn = sbt("memn", [128, KD, NM], BF16)
        kTp = sbt("kTp", [128, NH, NM], BF16)
        vp = sbt("vp", [128, 2, XAW], BF16)
        kvo = [sbt("kvo%d" % i, [128, XAW], F32) for i in range(2)]
        haloA = sbt("haloA", [128, NA, NJ, WA - 1], F32)
        haloB = sbt("haloB", [128, max(NB, 1), NJ, HPB], F32)
        ubbt = [sbt("ubb%d" % i, [128, UBW], BF16) for i in range(2)]
        vecs = sbt("vecs_sb", [128, NVEC], F32)
        idf = sbt("idf", [128, 128], F32)
        idb = sbt("idb", [128, 128], BF16)
        onesf = sbt("onesf", [128, 128], F32)
        onesb = sbt("onesb", [128, 128], BF16)
        Esel = sbt("Esel", [128, 8, 8], BF16)
        ps = [es.enter_context(nc.psum_tensor("ps%d" % i, [128, 512], F32)) for i in range(8)]

        ub = [scr[:, i * UBW:(i + 1) * UBW] for i in range(2)]
        o0 = 2 * UBW
        acc = scr[:, o0:o0 + TN]
        DG = o0 + TN
        DGW = max(2 * TN, 2 * TPE_B * 64)
        Abt = scr[:, DG:DG + TN]
        Btb = scr[:, DG + TN:DG + 2 * TN]
        diagt = [scr[:, DG + i * TPE_B * 64:DG + (i + 1) * TPE_B * 64].bitcast(BF16).rearrange("p (k m) -> p k m", m=128)
                 for i in range(2)] if TPE_B else [None, None]
        o1 = max(DG + DGW, 6400)
        thin = [scr[:, o1 + i * 512:o1 + (i + 1) * 512] for i in range(2)]
        outv = scr[:, 0:KD * TN].rearrange("p (o t) -> p o t", t=TN)
        NST = min(8, SCRW // D)
        stage = [scr[:, i * D:(i + 1) * D] for i in range(NST)]
        bfv = lambda w0, nw: scr[:, w0:w0 + nw].bitcast(BF16)
        NKV = 12
        kvstage = [bfv(i * 512, 512).rearrange("p (a e) -> p a e", e=XAW) for i in range(NKV)]
        kTs = [bfv(6144 + i * 512, 512).rearrange("p (h m) -> p h m", m=NM) for i in range(2)]
        pTs = bfv(7168, 512).rearrange("p (a e) -> p a e", e=512)
        qs = bfv(7680, 256).rearrange("p (h t) -> p h t", t=SC)
        szs = scr[:, 7936:8448].rearrange("p (h t) -> p h t", t=SC)
        sst = kvo
        sso = att
        stmp = rstdt[0]
        stsb = rstdt[1]
        qmask = [szt[0], szt[1]]

        cnt = {"bank": 0, "i": 0, "sq": 0}
        pend_ss = []
        dbg_out = {}

        def dbg(name, ap):
            if not DEBUG or name in dbg_out:
                return
            d = nc.dram_tensor("dbg_" + name, list(ap.shape), ap.dtype, kind="ExternalOutput").ap()
            dbg_out[name] = d
            P.op("sp", lambda e: e.dma_start(out=d, in_=ap), reads=[ap], dma=True)

        def V(off, n=1):
            return vecs[:, off:off + n]

        def mm(out, lhsT, rhs, start, stop, extra_r=(), extra_w=()):
            P.op("pe", lambda e: e.matmul(out, lhsT=lhsT, rhs=rhs, start=start, stop=stop),
                 reads=[lhsT, rhs], writes=[out])

        def tr(out, in_, ident):
            P.op("pe", lambda e: e.transpose(out=out, in_=in_, identity=ident), reads=[in_, ident], writes=[out])

        def act(out, in_, func, scale=1.0, bias=0.0):
            rd = [in_]
            if not isinstance(scale, (int, float)):
                rd.append(scale)
            if not isinstance(bias, (int, float)):
                rd.append(bias)
            P.op("act", lambda e: e.activation(out=out, in_=in_, func=func, bias=bias, scale=scale),
                 reads=rd, writes=[out])

        def tt(q, out, in0, in1, op):
            P.op(q, lambda e: e.tensor_tensor(out=out, in0=in0, in1=in1, op=op), reads=[in0, in1], writes=[out])

        def ts(q, out, in0, s1, op0, s2=None, op1=None):
            rd = [in0] + [s for s in (s1, s2) if s is not None and not isinstance(s, (int, float))]
            if op1 is None:
                P.op(q, lambda e: e.tensor_scalar(out=out, in0=in0, scalar1=s1, scalar2=None, op0=op0),
                     reads=rd, writes=[out])
            else:
                P.op(q, lambda e: e.tensor_scalar(out=out, in0=in0, scalar1=s1, scalar2=s2, op0=op0, op1=op1),
                     reads=rd, writes=[out])

        def stt(out, in0, scalar, in1, op0, op1):
            rd = [in0, in1] + ([] if isinstance(scalar, (int, float)) else [scalar])
            P.op("dve", lambda e: e.scalar_tensor_tensor(out=out, in0=in0, scalar=scalar, in1=in1, op0=op0, op1=op1),
                 reads=rd, writes=[out])

        def cp(q, out, in_):
            P.op(q, lambda e: e.tensor_copy(out=out, in_=in_), reads=[in_], writes=[out])

        def recip(out, in_):
            P.op("dve", lambda e: e.reciprocal(out=out, in_=in_), reads=[in_], writes=[out])

        def dma(q, out, in_):
            P.op(q, lambda e: e.dma_start(out=out, in_=in_), reads=[in_], writes=[out], dma=True)

        def memset(q, ap, val):
            P.op(q, lambda e: e.memset(ap, val), writes=[ap])

        def rsqrt_act(out, in_):
            act(out, in_, AF.Ln, bias=EPS)
            act(out, out, AF.Exp, scale=-0.5)

        memset("pool", onesf[:], 1.0)
        memset("pool", onesb[:], 1.0)
        P.op("pool", lambda e: e.affine_select(out=idf[:], in_=onesf[:], pattern=[[-1, 128]], compare_op=ALU.is_equal,
                                                fill=0.0, base=0, channel_multiplier=1),
             reads=[onesf[:]], writes=[idf[:]])
        cp("pool", idb[:], idf[:])
        memset("pool", Esel[:], 0.0)
        for r in range(8):
            memset("pool", Esel[:, r, r:r + 1], 1.0)
        dma("sp", vecs[:], vec_d[:])

        def load_transposed(dst3, src_rows, nrows, col0, ldq="sp"):
            i = cnt["i"]; cnt["i"] += 1
            stg = stage[i % NST]
            dma(ldq, stg[0:nrows, :], src_rows)
            for half in range(2):
                bank = ps[(2 * i + half) % 4]
                for k4 in range(4):
                    kc = half * 4 + k4
                    tr(bank[:, k4 * 128:k4 * 128 + nrows], stg[0:nrows, kc * 128:(kc + 1) * 128], idf[0:nrows, 0:nrows])
                src = bank[:, :].rearrange("p (k t) -> p k t", t=128)[:, :, 0:nrows]
                act(dst3[:, half * 4:half * 4 + 4, col0:col0 + nrows], src, AF.Copy)

        def store_transposed(dst_rows, src3, nrows, col0):
            i = cnt["i"]; cnt["i"] += 1
            stg = stage[i % NST]
            for half in range(2):
                bank = ps[(2 * i + half) % 4]
                for k4 in range(4):
                    kc = half * 4 + k4
                    tr(bank[0:nrows, k4 * 128:(k4 + 1) * 128], src3[:, kc, col0:col0 + nrows], idf[:, :])
                act(stg[0:nrows, half * 512:(half + 1) * 512], bank[0:nrows, :], AF.Copy)
            dma("sp", dst_rows, stg[0:nrows, :])

        for mc in range(2):
            load_transposed(memhT, mem_d[mc * 128:(mc + 1) * 128, :], 128, mc * 128)
        for kc in range(KD):
            act(sqt[kc % 2][:, 0:NM], memhT[:, kc, :], AF.Square, scale=1.0 / 32.0)
            mm(ps[4][:, 0:NM], onesb[:], sqt[kc % 2][:, 0:NM], kc == 0, kc == KD - 1)
        rsqrt_act(rstdt[0][:, 0:NM], ps[4][:, 0:NM])
        tt("dve", memhT[:], memhT[:], rstdt[0][:, 0:NM].unsqueeze(1).broadcast_to([128, KD, NM]), ALU.mult)

        def emit_memn(l):
            off = (V_A_MEM if l % 2 == 0 else V_B_MEM) + (l // 2) * 8
            tt("dve", memn[:], memhT[:], V(off, 8).unsqueeze(2).broadcast_to([128, KD, NM]), ALU.mult)

        pending = {}
        items = []

        def wsrc(w_d, l, r0, c0):
            return w_d[l, r0:r0 + 1024, c0:c0 + 128].rearrange("(kc p) m -> p kc m", p=128)

        for pz in range(NPASS):
            has_s = (pz == 0)
            ptl = [("p", i * PT, PT) for i in range(NPT)]
            stl = [("s", PC, SC)] if has_s else []
            ctilesA = ptl[:1] + stl + ptl[1:]
            ctilesB = stl + ptl
            ctiles = ctilesA
            ncols = PC + (SC if has_s else 0)
            last_pass = (pz == NPASS - 1)
            tok0 = pz * PC

            def prologue(pz=pz, has_s=has_s, tok0=tok0):
                for tb in range(PC // 128) if PC >= 128 else []:
                    load_transposed(x, xp_d[tok0 + tb * 128:tok0 + (tb + 1) * 128, :], 128, tb * 128,
                                    ldq=("pool" if pz > 0 else "sp"))
                if has_s:
                    load_transposed(x, xs_d[:, :], 128, PC)
            items.append(([], prologue))

            def emit_layer(l, pz=pz, has_s=has_s, last_pass=last_pass, ctilesA=ctilesA, ctilesB=ctilesB):
                ctiles = ctilesA if l % 2 == 0 else ctilesB
                isA = (l % 2 == 0)
                jl = l // 2
                win_d, wkv_d, wout_d = (awin_d, awkv_d, awout_d) if isA else (bwin_d, bwkv_d, bwout_d)
                vpre, vpost, vmem = (V_A_PRE, V_A_POST, V_A_MEM) if isA else (V_B_PRE, V_B_POST, V_B_MEM)
                W = WA if isA else WB
                HP = W - 1
                SW = HP + ST
                SOFF = HP + PC
                if isA:
                    c_h, c_b, c_c, c_z, c_q = 0, MIXW, 2 * MIXW, 3 * MIXW, 3 * MIXW + BRW
                else:
                    c_v, c_g, c_z, c_q = 0, MIXW, 2 * MIXW, 2 * MIXW + BRW
                cw_off = (V_A_CW + jl * NJ * WA) if isA else (V_B_CW + jl * NJ * WB)

                def stepM_pre():
                    if (pz == 0 and l == 0) or not PREMEM:
                        emit_memn(l)

                def stepM_k(slots, e, l=l, pz=pz):
                    w = slots[0]
                    bank = ps[e % 2]
                    for kc in range(KD):
                        mm(bank[:, 0:NM], w[:, kc, :], memn[:, kc, :], kc == 0, kc == KD - 1)
                    act(kTp[:, e, :], bank[:, 0:NM], AF.Copy)
                    if pz == 0:
                        for mc in range(2):
                            for kc in range(KD):
                                mm(ps[2 + mc][:, e * 128:(e + 1) * 128], memn[:, kc, mc * 128:(mc + 1) * 128], w[:, kc, :],
                                   kc == 0, kc == KD - 1)
                        if e == NH - 1:
                            for mc in range(2):
                                act(kvo[mc][:], ps[2 + mc][:], AF.Copy)
                                dma("sp", mk_d[l, mc * 128:(mc + 1) * 128, :], kvo[mc][:])

                def stepM_v(slots, e, l=l, pz=pz):
                    w = slots[0]
                    for mc in range(2):
                        for kc in range(KD):
                            mm(ps[2 + mc][:, e * 128:(e + 1) * 128], memn[:, kc, mc * 128:(mc + 1) * 128], w[:, kc, :],
                               kc == 0, kc == KD - 1)
                    if e == NH - 1:
                        for mc in range(2):
                            act(vp[:, mc, :], ps[2 + mc][:], AF.Copy)
                            if pz == 0:
                                act(kvo[mc][:], ps[2 + mc][:], AF.Copy)
                                dma("sp", mv_d[l, mc * 128:(mc + 1) * 128, :], kvo[mc][:])

                items.append(([], stepM_pre))
                for e in range(NH):
                    items.append(([("ring", wsrc(wkv_d, jl, 0, e * 128))], lambda s, e=e, f=stepM_k: f(s, e)))
                mv_idx = []
                for e in range(NH):
                    mv_idx.append(len(items))
                    items.append(([("ring", wsrc(wkv_d, jl, 0, XAW + e * 128))], lambda s, e=e, f=stepM_v: f(s, e)))

                post_prev = pending.pop("post", None)

                def stepN(only=None, jl=jl, vpre=vpre, ctiles=ctiles, HP=HP, SOFF=SOFF, pz=pz):
                    for ci, (kind, c0, w) in enumerate(ctiles):
                        if only is not None and ci != only:
                            continue
                        if post_prev is not None:
                            post_prev(c0)
                        bank = ps[4 + ci % 2]
                        for kc in range(KD):
                            i = cnt["i"]; cnt["i"] += 1
                            act(sqt[i % 2][:, 0:w], x[:, kc, c0:c0 + w], AF.Square, scale=1.0 / 32.0)
                            mm(bank[:, 0:w], onesb[:], sqt[i % 2][:, 0:w], kc == 0, kc == KD - 1)
                        rs = rstdt[ci % 2]
                        rsqrt_act(rs[:, 0:w], bank[:, 0:w])
                        for kc in range(KD):
                            stt(xn[:, kc, c0:c0 + w], x[:, kc, c0:c0 + w], V(vpre + jl * 8 + kc), rs[:, 0:w],
                                ALU.mult, ALU.mult)
                    if only is not None and only != len(ctiles) - 1:
                        return
                    if pz == 0:
                        for b2 in range(2):
                            memset("pool", ub[b2][:, 0:HP], 0.0)
                            if vpre == V_B_PRE and TPE_B:
                                memset("pool", ubbt[b2][:, 0:HP], 0.0)
                    dbg("xn_l%d_p%d" % (jl * 2 + (0 if vpre == V_A_PRE else 1), pz), xn[:])
                    dbg("x_l%d_p%d" % (jl * 2 + (0 if vpre == V_A_PRE else 1), pz), x[:])
                nt_ = len(ctiles)
                for k_ in range(1, nt_):
                    items.append(([], (lambda k_=k_: stepN(k_))))
                items.insert(mv_idx[0] + 1, ([], (lambda: stepN(0))))

                TPE = 0 if isA else TPE_B
                rotl = [5, 6, 7] if has_s else [4, 5, 6, 7]

                def rot_bank():
                    r = cnt["bank"]; cnt["bank"] += 1
                    return ps[rotl[r % len(rotl)]]
                hb_out = ps[3]
                halo_l = haloA[:, jl] if isA else haloB[:, jl]

                def ub_views(ubuf, kind, c0, w):
                    if kind == "p":
                        return ubuf[:, HP + c0:HP + c0 + w], (lambda k: ubuf[:, c0 + k:c0 + k + w])
                    us = ubuf[:, SOFF:SOFF + SB * SW].rearrange("p (s b) -> p b s", b=SB)
                    return us[:, :, HP:HP + ST], (lambda k: us[:, :, k:k + ST])

                def tview(ap2, kind, w):
                    return ap2 if kind == "p" else ap2.rearrange("p (b t) -> p b t", t=ST)

                sring = [kvo[0], kvo[1], rstdt[1]] if SRING3 else [kvo[0], kvo[1], kvo[0]]

                def head_dma(j):
                    if has_s:
                        st = sring[j % 3]
                        if isA:
                            nr = SB * HP
                            dma("sp", st[0:nr, 0:128], sa_d[jl, :, j * 128:(j + 1) * 128])
                        else:
                            nr = 4 * HP
                            dma("sp", st[0:nr, :].rearrange("r (k c) -> r k c", c=128),
                                sb_d[jl, :, j * 128:(j + 1) * 128].rearrange("(k r) c -> r k c", r=nr))

                def diag_build(j):
                    if TPE:
                        wv = vecs[:, cw_off + j * W:cw_off + j * W + TPE]
                        tt(DIAGQ, diagt[j % 2], idb[:].unsqueeze(1).broadcast_to([128, TPE, 128]),
                           wv.unsqueeze(2).broadcast_to([128, TPE, 128]), ALU.mult)

                def head_rest(j):
                    ubuf = ub[j % 2]
                    bufs = [ubuf] + ([ubbt[j % 2]] if TPE else [])
                    if pz > 0:
                        for bf in bufs:
                            act(bf[:, 0:HP], halo_l[:, j, 0:HP], AF.Copy)
                    if has_s:
                        st = sring[j % 3]
                        hb_in = rot_bank() if isA else ps[HBB]
                        if isA:
                            nr = SB * HP
                            tr(hb_in[:, 0:nr], st[0:nr, 0:128], idf[0:nr, 0:nr])
                        else:
                            nr = 4 * HP
                            for k in range(4):
                                tr(hb_in[:, k * nr:(k + 1) * nr], st[0:nr, k * 128:(k + 1) * 128], idf[0:nr, 0:nr])
                        for bf in bufs:
                            us = bf[:, SOFF:SOFF + SB * SW].rearrange("p (s b) -> p b s", b=SB)
                            act(us[:, :, 0:HP], hb_in[:, 0:SB * HP].rearrange("p (b r) -> p b r", r=HP), AF.Copy)

                def tail_a(j):
                    ubuf = ub[j % 2]
                    act(halo_l[:, j, 0:HP], ubuf[:, PC:PC + HP], AF.Copy)
                    if has_s:
                        us = ubuf[:, SOFF:SOFF + SB * SW].rearrange("p (s b) -> p b s", b=SB)
                        act(stmp[:, 0:SB * HP].rearrange("p (b r) -> p b r", r=HP), us[:, :, ST:ST + HP], AF.Copy)

                def tail_b(j):
                    cd, csd = (cap_d, cas_d) if isA else (cbp_d, cbs_d)
                    hb_out = rot_bank() if isA else ps[3]
                    if last_pass:
                        i = cnt["i"]; cnt["i"] += 1
                        so = sso[i % 2]
                        tr(hb_out[0:HP, 0:128], halo_l[:, j, 0:HP], idf[:, :])
                        act(so[0:HP, 0:128], hb_out[0:HP, 0:128], AF.Copy)
                        dma("sp", cd[jl, :, j * 128:(j + 1) * 128], so[0:HP, 0:128])
                    if has_s:
                        i = cnt["i"]; cnt["i"] += 1
                        so = sso[i % 2]
                        if isA:
                            nr = SB * HP
                            tr(hb_out[0:nr, 0:128], stmp[:, 0:nr], idf[:, :])
                            act(so[0:nr, 0:128], hb_out[0:nr, 0:128], AF.Copy)
                            dma("sp", csd[jl, :, j * 128:(j + 1) * 128], so[0:nr, 0:128])
                        else:
                            nr = 4 * HP
                            for k in range(4):
                                tr(hb_out[0:nr, k * 128:(k + 1) * 128], stmp[:, k * nr:(k + 1) * nr], idf[:, :])
                            act(so[0:nr, :], hb_out[0:nr, :], AF.Copy)
                            dma("sp", csd[jl, :, j * 128:(j + 1) * 128].rearrange("(k r) c -> r k c", r=nr),
                                so[0:nr, :].rearrange("r (k c) -> r k c", c=128))

                def conv_taps(ubuf, j, tiles, k0):
                    vs = []
                    for (kind, c0, w) in tiles:
                        if kind == "p":
                            hw_ = max(w // 2, 1)
                            for s0 in range(0, w, hw_):
                                vs.append((acc[:, c0 + s0:c0 + s0 + hw_],
                                           (lambda k, a0=c0 + s0, ww=hw_: ubuf[:, a0 + k:a0 + k + ww])))
                        else:
                            _, rd = ub_views(ubuf, kind, c0, w)
                            vs.append((tview(acc[:, c0:c0 + w], kind, w), rd))
                    for k in range(k0, W):
                        wk = V(cw_off + j * W + k)
                        for a_, rd in vs:
                            if k == k0:
                                ts("dve", a_, rd(k), wk, ALU.mult)
                            else:
                                stt(a_, rd(k), wk, a_, ALU.mult, ALU.add)

                def pe_taps(j, tiles):
                    ubb = ubbt[j % 2]
                    for ci, (kind, c0, w) in enumerate(tiles):
                        if kind == "p":
                            rd = (lambda k, c0=c0, w=w: ubb[:, c0 + k:c0 + k + w])
                        else:
                            rd = (lambda k: ubb[:, SOFF + k * SB:SOFF + (k + ST) * SB])
                        for k in range(TPE):
                            mm(ps[4 + ci][:, 0:w], diagt[j % 2][:, k, :], rd(k), k == 0, k == TPE - 1)

                def item_pre(j):
                    if j == 0:
                        head_dma(0)
                        diag_build(0)
                        head_rest(0)
                    if j + 1 < NJ:
                        head_dma(j + 1)
                        if HEADPOS < 0:
                            head_rest(j + 1)

                def item_tile(j, ti):
                    hp_ = HEADPOS if isA else len(ctiles) - 1
                    if HEADPOS >= 0 and ti == min(hp_, len(ctiles) - 1) and j + 1 < NJ:
                        head_rest(j + 1)

                def item_mid(j):
                    if j + 1 < NJ:
                        diag_build(j + 1)
                    if j > 0:
                        tail_b(j - 1)

                def item_post(j):
                    tail_a(j)
                    if j == NJ - 1:
                        tail_b(j)

                if isA:
                    def p1A(slots, j):
                        wh, wb_, wc, wz = slots
                        ubuf = ub[j % 2]
                        item_pre(j)
                        post = []
                        for ti, (kind, c0, w) in enumerate(ctiles):
                            i = cnt["i"]; cnt["i"] += 1
                            if kind == "p":
                                bh, bc = rot_bank()[:, 0:w], rot_bank()[:, 0:w]
                                pi_ = c0 // PT
                                bb, bz = ps[2 * pi_][:, 0:w], ps[2 * pi_ + 1][:, 0:w]
                            else:
                                bh, bc, bz = rot_bank()[:, 0:w], rot_bank()[:, 0:w], rot_bank()[:, 0:w]
                                bb = ps[4][:, 0:w]
                            for wt_, bk in ((wh, bh), (wc, bc), (wb_, bb), (wz, bz)):
                                for kc in range(KD):
                                    mm(bk, wt_[:, kc, :], xn[:, kc, c0:c0 + w], kc == 0, kc == KD - 1)
                            th = thin[i % 2]
                            act(th[:, 0:w], bh, AF.Copy)
                            uw, _ = ub_views(ubuf, kind, c0, w)
                            tt("dve", uw, tview(bc, kind, w), tview(th[:, 0:w], kind, w), ALU.mult)
                            if kind != "p":
                                act(szsA[:, 0:w], bz, AF.Silu)
                                bz = None
                            post.append((c0, w, bb, bz))
                            item_tile(j, ti)
                        item_mid(j)
                        conv_taps(ubuf, j, ctiles, 0)
                        for (c0, w, bb, bz) in post:
                            tt("dve", acc[:, c0:c0 + w], acc[:, c0:c0 + w], bb, ALU.mult)
                        for (c0, w, bb, bz) in post:
                            i = cnt["i"]; cnt["i"] += 1
                            sz = szt[i % 2]
                            if bz is None:
                                sz = szsA
                            else:
                                act(sz[:, 0:w], bz, AF.Silu)
                            tt("dve", br[:, j, c0:c0 + w], acc[:, c0:c0 + w], sz[:, 0:w], ALU.mult)
                        if j == 0:
                            dbg("ubA_l%d_p%d" % (l, pz), ubuf)
                            dbg("accA_l%d_p%d" % (l, pz), acc)
                        item_post(j)
                    for j in range(NJ):
                        items.append(([("ring", wsrc(win_d, jl, 0, c + j * 128)) for c in (c_h, c_b, c_c, c_z)],
                                      lambda s, j=j, f=p1A: f(s, j)))
                else:
                    def ln_stat_mm(j):
                        order = sorted(range(len(ctiles)), key=lambda c_: -ctiles[c_][2])
                        for n_, ci in enumerate(order):
                            kind, c0, w = ctiles[ci]
                            first = (j == 0 and n_ == 0)
                            last = (j == NJ - 1 and n_ == len(order) - 1)
                            mm(ps[7][0:8, 0:w], Esel[:, ci, :], br[:, j, c0:c0 + w], first, False)
                            mm(ps[7][0:8, 0:w], Esel[:, 4 + ci, :], sq3[ci][:, 0:w], False, last)

                    def post_act(j):
                        for ci, (kind, c0, w) in enumerate(ctiles):
                            cb = V(V_B_CB + jl * NJ + j)
                            act(br[:, j, c0:c0 + w], acc[:, c0:c0 + w], AF.Identity, bias=cb)
                            act(sq3[ci][:, 0:w], acc[:, c0:c0 + w], AF.Square, bias=cb)

                    def p1B(slots, j):
                        wv, wg = slots
                        ubuf = ub[j % 2]
                        item_pre(j)
                        for ti, (kind, c0, w) in enumerate(ctiles):
                            i = cnt["i"]; cnt["i"] += 1
                            b0 = 2 * (ti % 2)
                            bv, bg = ps[b0], ps[b0 + 1]
                            for wt_, bk in ((wg, bg), (wv, bv)):
                                for kc in range(KD):
                                    mm(bk[:, 0:w], wt_[:, kc, :], xn[:, kc, c0:c0 + w], kc == 0, kc == KD - 1)
                            th = thin[i % 2]
                            act(th[:, 0:w], bg[:, 0:w], AF.Sigmoid)
                            uw, _ = ub_views(ubuf, kind, c0, w)
                            tt("dve", uw, tview(bv[:, 0:w], kind, w), tview(th[:, 0:w], kind, w), ALU.mult)
                            if TPE:
                                uwb, _ = ub_views(ubbt[j % 2], kind, c0, w)
                                act(uwb, uw, AF.Copy)
                            if ti == min(1, len(ctiles) - 1) and j > 0 and DEFER:
                                post_act(j - 1)
                            item_tile(j, ti)
                        item_mid(j)
                        if TPE:
                            pe_taps(j, ctiles)
                        if j > 0 and DEFER:
                            ln_stat_mm(j - 1)
                        conv_taps(ubuf, j, ctiles, TPE)
                        if TPE:
                            for ci, (kind, c0, w) in enumerate(ctiles):
                                if kind == "p":
                                    tt("dve", acc[:, c0:c0 + w], acc[:, c0:c0 + w], ps[4 + ci][:, 0:w], ALU.add)
                                else:
                                    a3 = acc[:, c0:c0 + w].rearrange("p (b t) -> p b t", t=ST)
                                    tt("dve", a3, a3, ps[4 + ci][:, 0:w].rearrange("p (t b) -> p b t", b=SB), ALU.add)
                        item_post(j)
                        if j == NJ - 1 or not DEFER:
                            post_act(j)
                            ln_stat_mm(j)
                    for j in range(NJ):
                        items.append(([("ring", wsrc(win_d, jl, 0, c + j * 128)) for c in (c_v, c_g)],
                                      lambda s, j=j, f=p1B: f(s, j)))

                    def lnstats(ctiles=ctiles):
                        act(stsb[0:8, :], ps[7][0:8, :], AF.Copy)
                        for ci, (kind, c0, w) in enumerate(ctiles):
                            for si, r in enumerate((ci, 4 + ci)):
                                ts("dve", qmask[si][0:8, 0:w], stsb[0:8, 0:w], idf[0:8, r:r + 1], ALU.mult)
                                mm(ps[4 + si][:, 0:w], onesf[0:8, :], qmask[si][0:8, 0:w], True, True)
                            mean = att[0]
                            ts("dve", mean[:, 0:w], ps[4][:, 0:w], 1.0 / MIXW, ALU.mult)
                            msq = att[1]
                            tt("dve", msq[:, 0:w], mean[:, 0:w], mean[:, 0:w], ALU.mult)
                            stt(msq[:, 0:w], ps[5][:, 0:w], 1.0 / MIXW, msq[:, 0:w], ALU.mult, ALU.subtract)
                            rsqrt_act(Abt[:, c0:c0 + w], msq[:, 0:w])
                            stt(Btb[:, c0:c0 + w], mean[:, 0:w], -1.0, Abt[:, c0:c0 + w], ALU.mult, ALU.mult)
                    items.append(([], lnstats))

                    def p2B(slots, j, ctiles=ctiles, jl=jl):
                        wz = slots[0]
                        szb = [szt[0], szt[1], rstdt[1]]
                        t1b = [att[0], att[1], rstdt[0]]
                        for ti, (kind, c0, w) in enumerate(ctiles):
                            i = cnt["i"]; cnt["i"] += 1
                            bz = ps[i % 4]
                            for kc in range(KD):
                                mm(bz[:, 0:w], wz[:, kc, :], xn[:, kc, c0:c0 + w], kc == 0, kc == KD - 1)
                            act(szb[ti][:, 0:w], bz[:, 0:w], AF.Silu)
                            tt("dve", t1b[ti][:, 0:w], br[:, j, c0:c0 + w], Abt[:, c0:c0 + w], ALU.mult)
                        for ti, (kind, c0, w) in enumerate(ctiles):
                            tt("dve", t1b[ti][:, 0:w], t1b[ti][:, 0:w], Btb[:, c0:c0 + w], ALU.add)
                        for ti, (kind, c0, w) in enumerate(ctiles):
                            act(t1b[ti][:, 0:w], t1b[ti][:, 0:w], AF.Silu, scale=V(V_B_LG + jl * NJ + j),
                                bias=V(V_B_LB + jl * NJ + j))
                        for ti, (kind, c0, w) in enumerate(ctiles):
                            tt("dve", br[:, j, c0:c0 + w], t1b[ti][:, 0:w], szb[ti][:, 0:w], ALU.mult)
                    for j in range(NJ):
                        items.append(([("ring", wsrc(win_d, jl, 0, c_z + j * 128))], lambda s, j=j, f=p2B: f(s, j)))

                def p3(slots, h, ctiles=ctiles):
                    wq, wz = slots
                    scale = float(128 ** -0.5)
                    A0 = 0 if h % 2 == 0 else 4
                    B0 = 4 - A0
                    pts = [t for t in ctiles if t[0] == "p"]
                    for ti, (kind, c0, w) in enumerate(ctiles):
                        bq, bz = ps[A0 + 2 * (ti % 2)][:, 0:w], ps[A0 + 2 * (ti % 2) + 1][:, 0:w]
                        for wt_, bk in ((wq, bq), (wz, bz)):
                            for kc in range(KD):
                                mm(bk, wt_[:, kc, :], xn[:, kc, c0:c0 + w], kc == 0, kc == KD - 1)
                        if kind == "p":
                            pi_ = (c0 // PT) % 2
                            qT, sz, zc = qTt[pi_][:, 0:w], szt[pi_][:, 0:w], rstdt[pi_][:, 0:w]
                        else:
                            qT, sz, zc = qs[:, h, :], szs[:, h, :], szsA[:, 0:w]
                        cp("dve", qT, bq)
                        act(sz, bz, AF.Exp, scale=-1.0)
                        act(sz, sz, AF.Ln, bias=1.0)
                        act(sz, sz, AF.Exp, scale=-1.0)
                        tt("dve", sz, sz, bz, ALU.mult)
                    for n_, (kind, c0, w) in enumerate(pts):
                        pi_ = (c0 // PT) % 2
                        for mc in range(2):
                            bk = ps[B0 + 2 * (n_ % 2) + mc]
                            mm(bk[:, 0:w], kTp[:, h, mc * 128:(mc + 1) * 128], qTt[pi_][:, 0:w], True, True)
                            act(pTt[pi_][:, mc, 0:w], bk[:, 0:w], AF.Exp, scale=scale)
                    for n_, (kind, c0, w) in enumerate(pts):
                        pi_ = (c0 // PT) % 2
                        pT = pTt[pi_]
                        bd, bo_ = ps[A0 + 2 * (n_ % 2)], ps[A0 + 2 * (n_ % 2) + 1]
                        for mc in range(2):
                            mm(bd[:, 0:w], onesb[:], pT[:, mc, 0:w], mc == 0, mc == 1)
                        for mc in range(2):
                            mm(bo_[:, 0:w], vp[:, mc, h * 128:(h + 1) * 128], pT[:, mc, 0:w], mc == 0, mc == 1)
                        rd = att[pi_]
                        act(rd[:, 0:w], bd[:, 0:w], AF.Ln)
                        act(rd[:, 0:w], rd[:, 0:w], AF.Exp, scale=-1.0)
                        tt("dve", rd[:, 0:w], rd[:, 0:w], szt[pi_][:, 0:w], ALU.mult)
                        tt("dve", br[:, NJ + h, c0:c0 + w], bo_[:, 0:w], rd[:, 0:w], ALU.mult)
                for h in range(NH):
                    items.append(([("ring", wsrc(win_d, jl, 0, c_q + h * 128)),
                                   ("ring", wsrc(win_d, jl, 0, c_z + (NJ + h) * 128))],
                                  lambda s, h=h, f=p3: f(s, h), "p3"))

                if has_s:
                    scale = float(128 ** -0.5)

                    def samp_scores(b):
                        kt = kTs[b % 2]
                        for h in range(NH):
                            for mc in range(2):
                                col = (b * NH + h) * ST
                                mm(ps[2 + mc][:, col:col + ST], kt[:, h, mc * 128:(mc + 1) * 128],
                                   qs[:, h, b * ST:(b + 1) * ST], True, True)

                    def samp_k(slots, b):
                        kr = slots[0]
                        bankb = ps[b % 2][:].bitcast(BF16)
                        for h in range(NH):
                            for mc in range(2):
                                tr(bankb[:, (h * 2 + mc) * 128:(h * 2 + mc + 1) * 128], kr[:, mc, h * 128:(h + 1) * 128], idb[:])
                        act(kTs[b % 2].rearrange("p h m -> p (h m)"), bankb[:, :], AF.Copy)
                        if b > 0:
                            samp_scores(b - 1)
                        if b == SB - 1:
                            samp_scores(b)
                            for mc in range(2):
                                act(pTs[:, mc, :], ps[2 + mc][:, :], AF.Exp, scale=scale)
                            for mc in range(2):
                                mm(ps[4][:, :], onesb[:], pTs[:, mc, :], mc == 0, mc == 1)

                    def samp_v(slots, b):
                        vr = slots[0]
                        for h in range(NH):
                            col = (b * NH + h) * ST
                            for mc in range(2):
                                mm(ps[5][:, col:col + ST], vr[:, mc, h * 128:(h + 1) * 128], pTs[:, mc, col:col + ST],
                                   mc == 0, mc == 1)
                        if b == SB - 1:
                            rd = att[0]
                            act(rd[:], ps[4][:], AF.Ln)
                            act(rd[:], rd[:], AF.Exp, scale=-1.0)
                            for h in range(NH):
                                rv = rd[:].rearrange("p (b h t) -> p b h t", h=NH, t=ST)[:, :, h, :]
                                ov = ps[5][:].rearrange("p (b h t) -> p b h t", h=NH, t=ST)[:, :, h, :]
                                t2 = att[1][:, 0:SC].rearrange("p (b t) -> p b t", t=ST)
                                tt("dve", t2, rv, szs[:, h, :].rearrange("p (b t) -> p b t", t=ST), ALU.mult)
                                tt("dve", br[:, NJ + h, PC:PC + SC].rearrange("p (b t) -> p b t", t=ST), ov, t2, ALU.mult)
                    for b in range(SB):
                        items.append(([("buf", None, ck_d[l, b].rearrange("(mc p) e -> p mc e", p=128))],
                                      lambda s, b=b, f=samp_k: f(s, b), "p3"))
                    for b in range(SB):
                        items.append(([("buf", None, cv_d[l, b].rearrange("(mc p) e -> p mc e", p=128))],
                                      lambda s, b=b, f=samp_v: f(s, b), "p3"))

                def stepO(slots, o, ctiles=ctiles, jl=jl, vpost=vpost):
                    w0, w1 = slots
                    if o == 0 and PREMEM:
                        emit_memn((l + 1) % DEPTH)
                    for ci, (kind, c0, w) in enumerate(ctiles):
                        i = cnt["i"]; cnt["i"] += 1
                        bo = ps[i % 4]
                        for kc in range(16):
                            wt_ = w0 if kc < 8 else w1
                            mm(bo[:, 0:w], wt_[:, kc % 8, :], br[:, kc, c0:c0 + w], kc == 0, kc == 15)
                        if pend_ss:
                            pend_ss.pop(0)()
                        act(outv[:, o, c0:c0 + w], bo[:, 0:w], AF.Identity, scale=V(vpost + jl * 8 + o))
                        k_ = cnt["sq"]; cnt["sq"] += 1
                        sq = sq3[k_ % 3]
                        act(sq[:, 0:w], bo[:, 0:w], AF.Square, scale=1.0 / 32.0)
                        pend_ss.append(lambda ci=ci, w=w, sq=sq, o=o: mm(ps[5 + ci][:, 0:w], onesb[:], sq[:, 0:w],
                                                                       o == 0, o == KD - 1))
                    if o == KD - 1:
                        while pend_ss:
                            pend_ss.pop(0)()

                def stepO_post(only=None):
                    for ci, (kind, c0, w) in enumerate(ctiles):
                        if only is not None and c0 != only:
                            continue
                        rs = rstdt[ci % 2]
                        rsqrt_act(rs[:, 0:w], ps[5 + ci][:, 0:w])
                        for oo in range(KD):
                            q_ = "pool" if (oo >= KD - 1 and POOLPOST) else "dve"
                            tt(q_, outv[:, oo, c0:c0 + w], outv[:, oo, c0:c0 + w], rs[:, 0:w], ALU.mult)
                            tt(q_, x[:, oo, c0:c0 + w], x[:, oo, c0:c0 + w], outv[:, oo, c0:c0 + w], ALU.add)
                pending["post"] = stepO_post
                items.append(([], lambda l=l, pz=pz: dbg("br_l%d_p%d" % (l, pz), br[:])))
                for o in range(KD):
                    items.append(([("ring", wsrc(wout_d, jl, 0, o * 128)), ("ring", wsrc(wout_d, jl, 1024, o * 128))],
                                  lambda s, o=o, f=stepO: f(s, o)))

            for l in range(DEPTH):
                emit_layer(l)
            items.append(([], pending.pop("post")))

            def epilogue(pz=pz, has_s=has_s, tok0=tok0):
                for tb in range(PC // 128):
                    store_transposed(yp_d[tok0 + tb * 128:tok0 + (tb + 1) * 128, :], x, 128, tb * 128)
                if has_s:
                    store_transposed(ys_d[:, :], x, 128, PC)
            items.append(([], epilogue))

        rstate = {"next": 0, "kv": 0}
        issued = [None] * len(items)

        def issue(idx):
            loads = items[idx][0]
            slots = []
            for ld in loads:
                if ld[0] == "ring":
                    s = rstate["next"] % NR
                    rstate["next"] += 1
                    dst = ring[:, s]
                    dma("pool", dst, ld[1])
                    slots.append(dst)
                else:
                    dst = kvstage[rstate["kv"] % NKV]
                    rstate["kv"] += 1
                    dma("pool", dst, ld[2])
                    slots.append(dst)
            issued[idx] = slots

        def nring(idx):
            return sum(1 for ld in items[idx][0] if ld[0] == "ring")

        nxt = 0
        live = []
        for i in range(len(items)):
            while nxt < len(items):
                if nxt <= i:
                    pass
                else:
                    units = sum(u for _, u in live) + nring(nxt)
                    nbuf = sum(1 for k in range(i, nxt) if any(ld[0] == "buf" for ld in items[k][0]))
                    isbuf = any(ld[0] == "buf" for ld in items[nxt][0])
                    if units > NR or nxt - i > 24 or (isbuf and (nbuf >= NKV or len(items[i]) < 3)):
                        break
                issue(nxt)
                live.append((nxt, nring(nxt)))
                nxt += 1
            loads, fn = items[i][0], items[i][1]
            if loads:
                fn(issued[i])
            else:
                fn()
            live = [(k, u) for (k, u) in live if k != i]

        P.finish()
        P.emit(nc, es)
    return nc


def _pack_vecs(inp, DEPTH):
    NA = (DEPTH + 1) // 2
    NB = DEPTH // 2
    v = np.zeros((128, NVEC), np.float32)

    def put8(off, arr, n):
        for l in range(n):
            v[:, off + l * 8:off + (l + 1) * 8] = arr[l].reshape(8, 128).T
    put8(V_A_PRE, inp["a_norm_pre"], NA)
    put8(V_A_POST, inp["a_norm_post"], NA)
    put8(V_A_MEM, inp["a_mem_norm"], NA)
    put8(V_B_PRE, inp["b_norm_pre"], NB)
    put8(V_B_POST, inp["b_norm_post"], NB)
    put8(V_B_MEM, inp["b_mem_norm"], NB)
    for l in range(NA):
        w = inp["a_conv_w"][l]
        v[:, V_A_CW + l * NJ * WA:V_A_CW + (l + 1) * NJ * WA] = \
            w.reshape(WA, NJ, 128).transpose(2, 1, 0).reshape(128, NJ * WA)
    for l in range(NB):
        w = inp["b_conv_w"][l]
        v[:, V_B_CW + l * NJ * WB:V_B_CW + (l + 1) * NJ * WB] = \
            w.reshape(WB, NJ, 128).transpose(2, 1, 0).reshape(128, NJ * WB)
        for off, key in ((V_B_CB, "b_conv_b"), (V_B_LG, "b_ln_g"), (V_B_LB, "b_ln_b")):
            v[:, off + l * NJ:off + (l + 1) * NJ] = inp[key][l].reshape(NJ, 128).T
    return v


_CACHE = {}
DEBUG = False
BUILD_KW = {}
DBG_RES = {}


def kernel(**inp):
    inp = {k: np.asarray(v) for k, v in inp.items()}
    NB_, SEQ, _ = inp["x_prompt"].shape
    DEPTH = inp["cache_mem_k"].shape[0]
    ncores = NB_
    key = (SEQ, DEPTH)
    if key not in _CACHE:
        _CACHE[key] = build_program(SEQ=SEQ, DEPTH=DEPTH, **BUILD_KW)
    nc = _CACHE[key]
    NA = (DEPTH + 1) // 2
    NB = DEPTH // 2
    vecs = _pack_vecs(inp, DEPTH)
    f = lambda a: np.ascontiguousarray(a, dtype=np.float32)
    shared = {k: f(inp[k]) for k in ("a_w_in", "a_w_kv", "a_w_out", "b_w_in", "b_w_kv", "b_w_out")}
    in_maps = []
    for c in range(ncores):
        sl = slice(c * SB, (c + 1) * SB)
        m = dict(shared)
        m["xp"] = f(inp["x_prompt"][c])
        m["xs"] = f(inp["x_sample"][sl].reshape(SC, D))
        m["mem"] = f(inp["mem_prompt"][c])
        m["ck"] = f(inp["cache_mem_k"][:, sl].reshape(DEPTH, SB, NM, XAW))
        m["cv"] = f(inp["cache_mem_v"][:, sl].reshape(DEPTH, SB, NM, XAW))
        m["sa"] = f(inp["state_conv_a"][:, sl].reshape(NA, SB * (WA - 1), MIXW))
        m["sb"] = f(inp["state_conv_b"][:, sl].reshape(NB, SB * (WB - 1), MIXW))
        m["vecs"] = vecs
        in_maps.append(m)
    res = run_bass_kernel_spmd(nc, in_maps, core_ids=list(range(ncores)))
    R = res.results
    if DEBUG:
        DBG_RES.update({k: v for k, v in R[0].items() if k.startswith("dbg_")})
    y_p = np.stack([R[c]["yp"] for c in range(ncores)])
    y_s = np.concatenate([R[c]["ys"].reshape(SB, ST, D) for c in range(ncores)], axis=0)
    mk = np.stack([R[c]["mk"].reshape(DEPTH, NM, NH, 128) for c in range(ncores)], axis=1)
    mv = np.stack([R[c]["mv"].reshape(DEPTH, NM, NH, 128) for c in range(ncores)], axis=1)
    cap = np.stack([R[c]["cap"] for c in range(ncores)], axis=1)
    cbp = np.stack([R[c]["cbp"] for c in range(ncores)], axis=1)
    cas = np.concatenate([R[c]["cas"].reshape(NA, SB, WA - 1, MIXW) for c in range(ncores)], axis=1)
    cbs = np.concatenate([R[c]["cbs"].reshape(NB, SB, WB - 1, MIXW) for c in range(ncores)], axis=1)
    out = (y_p, y_s, mk, mv, cap, cbp, cas, cbs)
    return tuple(np.ascontiguousarray(o, dtype=np.float32) for o in out)
```

```python
from contextlib import ExitStack
import numpy as np
import concourse.bass as bass
import concourse.mybir as mybir
from concourse.bass_utils import run_bass_kernel_spmd

F32 = mybir.dt.float32
BF16 = mybir.dt.bfloat16
AF = mybir.ActivationFunctionType
ALU = mybir.AluOpType

_DT_SIZE = {"float32": 4, "bfloat16": 2, "float32r": 4, "int32": 4, "uint32": 4,
            "float16": 2, "int16": 2, "uint16": 2, "int8": 1, "uint8": 1}


def _esz(dt):
    return _DT_SIZE[str(dt).split(".")[-1]]


class _Rec:
    __slots__ = ("plo", "phi", "blo", "bhi", "writer", "readers", "dma_readers")

    def __init__(self, plo, phi, blo, bhi, writer):
        self.plo, self.phi, self.blo, self.bhi = plo, phi, blo, bhi
        self.writer = writer
        self.readers = {}
        self.dma_readers = []


class _Op:
    __slots__ = ("q", "idx", "fn", "is_dma", "deps", "needs_inc", "cnt", "dsem", "dtarget")

    def __init__(self, q, idx, fn, is_dma):
        self.q, self.idx, self.fn, self.is_dma = q, idx, fn, is_dma
        self.deps = []
        self.needs_inc = False
        self.cnt = 0
        self.dsem = -1
        self.dtarget = 0


class Prog:
    QUEUES = ("pe", "act", "dve", "pool", "sp")
    NDMASEM = 8
    SAME_DIST = 3

    def __init__(self):
        self.ops = {q: [] for q in self.QUEUES}
        self.recs = {}
        self.dma_count = {q: 0 for q in self.QUEUES}
        self.dma_hist = {q: [] for q in self.QUEUES}

    @staticmethod
    def region(ap):
        t = ap.tensor
        if type(t).__name__.startswith("DRam"):
            return None
        dims = ap.ap
        row = dims[0][0]
        off = int(ap.offset)
        if row > 0:
            p0, f0 = off // row, off % row
        else:
            p0, f0 = 0, off
        npart = dims[0][1]
        span = 1
        for s, c in dims[1:]:
            span += (c - 1) * abs(s)
        e = _esz(ap.dtype)
        if type(t).__name__.startswith("PSum"):
            return (t.name, 0, 128, 0, 2048)
        return (t.name, p0, p0 + npart, f0 * e, (f0 + span) * e)

    def _access(self, op, ap, is_write):
        reg = self.region(ap)
        if reg is None:
            return
        name, plo, phi, blo, bhi = reg
        recs = self.recs.setdefault(name, [])
        deps = op.deps
        if is_write:
            keep = []
            for r in recs:
                if r.plo < phi and plo < r.phi and r.blo < bhi and blo < r.bhi:
                    if r.writer is not None:
                        deps.append(r.writer)
                    deps.extend(r.readers.values())
                    deps.extend(r.dma_readers)
                    if plo <= r.plo and r.phi <= phi and blo <= r.blo and r.bhi <= bhi:
                        continue
                keep.append(r)
            keep.append(_Rec(plo, phi, blo, bhi, op))
            self.recs[name] = keep
        else:
            for r in recs:
                if r.plo < phi and plo < r.phi and r.blo < bhi and blo < r.bhi:
                    if r.writer is not None:
                        deps.append(r.writer)
                    if op.is_dma:
                        r.dma_readers.append(op)
                    else:
                        r.readers[op.q] = op

    def op(self, q, fn, reads=(), writes=(), dma=False):
        lst = self.ops[q]
        o = _Op(q, len(lst), fn, dma)
        for ap in reads:
            self._access(o, ap, False)
        for ap in writes:
            self._access(o, ap, True)
        if dma:
            k = self.dma_count[q]
            self.dma_count[q] = k + 1
            o.dsem = k % self.NDMASEM
            o.dtarget = 16 * (k // self.NDMASEM + 1)
            hist = self.dma_hist[q]
            if k >= self.NDMASEM:
                o.deps.append(hist[k - self.NDMASEM])
            hist.append(o)
        nd = []
        seen = set()
        for d in o.deps:
            if d is o or id(d) in seen:
                continue
            seen.add(id(d))
            if (not d.is_dma) and d.q == q and (not dma):
                if q == "pe":
                    continue
                if o.idx - d.idx > self.SAME_DIST:
                    continue
            if not d.is_dma:
                d.needs_inc = True
            nd.append(d)
        o.deps = nd
        lst.append(o)
        return o

    def finish(self):
        o = _Op("sp", len(self.ops["sp"]), None, False)
        for q in self.QUEUES:
            o.deps.extend(self.dma_hist[q][-self.NDMASEM:])
        self.ops["sp"].append(o)

    def emit(self, nc, es):
        for q in self.QUEUES:
            c = 0
            for o in self.ops[q]:
                if o.needs_inc:
                    c += 1
                o.cnt = c
        esem = {q: es.enter_context(nc.semaphore("e_" + q)) for q in self.QUEUES}
        dsem = {q: [es.enter_context(nc.semaphore("d_%s%d" % (q, i))) for i in range(self.NDMASEM)]
                for q in self.QUEUES if self.dma_count[q] > 0}
        block = es.enter_context(nc.Block())

        def mk(q):
            def body(eng):
                waited = {f: 0 for f in self.QUEUES}
                dwaited = {}
                for o in self.ops[q]:
                    need = {}
                    for d in o.deps:
                        if d.is_dma:
                            key = (d.q, d.dsem)
                            if dwaited.get(key, 0) < d.dtarget:
                                dwaited[key] = d.dtarget
                                eng.wait_ge(dsem[d.q][d.dsem], d.dtarget)
                        elif d.cnt > need.get(d.q, 0):
                            need[d.q] = d.cnt
                    for f, c in need.items():
                        if c > waited[f]:
                            waited[f] = c
                            eng.wait_ge(esem[f], c)
                    if o.fn is None:
                        continue
                    ins = o.fn(eng)
                    if o.is_dma:
                        ins.then_inc(dsem[q][o.dsem], 16)
                    elif o.needs_inc:
                        ins.then_inc(esem[q], 1)
            return body

        block.tensor(mk("pe"))
        block.scalar(mk("act"))
        block.vector(mk("dve"))
        block.gpsimd(mk("pool"))
        block.sync(mk("sp"))


D = 1024
KD = 8
MIXW = 1536
NJ = 12
BRW = 2048
XAW = 512
NH = 4
A_IN = 3 * MIXW + BRW + XAW
B_IN = 2 * MIXW + BRW + XAW
NM = 256
SB = 16
ST = 8
SC = SB * ST
EPS = 1e-6
WA = 3
WB = 31

V_A_PRE, V_A_POST, V_A_MEM = 0, 16, 32
V_B_PRE, V_B_POST, V_B_MEM = 48, 64, 80
V_A_CW = 96
V_B_CW = V_A_CW + 2 * NJ * WA
V_B_CB = V_B_CW + 2 * NJ * WB
V_B_LG = V_B_CB + 2 * NJ
V_B_LB = V_B_LG + 2 * NJ
NVEC = V_B_LB + 2 * NJ


def build_program(SEQ=2048, DEPTH=4, NPASS=2, NR=12, TPE_B=27, SRING3=True, HBB=2, HEADPOS=0, POOLPOST=True, DEFER=True, PREMEM=True, DIAGQ="dve"):
    PC = SEQ // NPASS
    TN = PC + SC
    PT = min(512, PC)
    NPT = PC // PT
    HPB = WB - 1
    UBW = HPB + PC + SB * (HPB + ST)
    NA = (DEPTH + 1) // 2
    NB = DEPTH // 2

    nc = bass.Bass("TRN2", target_bir_lowering=False)
    dram = lambda n, s, k: nc.dram_tensor(n, s, F32, kind=k).ap()
    xp_d = dram("xp", [SEQ, D], "ExternalInput")
    xs_d = dram("xs", [SC, D], "ExternalInput")
    mem_d = dram("mem", [NM, D], "ExternalInput")
    ck_d = dram("ck", [DEPTH, SB, NM, XAW], "ExternalInput")
    cv_d = dram("cv", [DEPTH, SB, NM, XAW], "ExternalInput")
    sa_d = dram("sa", [NA, SB * (WA - 1), MIXW], "ExternalInput")
    sb_d = dram("sb", [max(NB, 1), SB * HPB, MIXW], "ExternalInput")
    vec_d = dram("vecs", [128, NVEC], "ExternalInput")
    awin_d = dram("a_w_in", [NA, D, A_IN], "ExternalInput")
    awkv_d = dram("a_w_kv", [NA, D, 2 * XAW], "ExternalInput")
    awout_d = dram("a_w_out", [NA, BRW, D], "ExternalInput")
    bwin_d = dram("b_w_in", [max(NB, 1), D, B_IN], "ExternalInput")
    bwkv_d = dram("b_w_kv", [max(NB, 1), D, 2 * XAW], "ExternalInput")
    bwout_d = dram("b_w_out", [max(NB, 1), BRW, D], "ExternalInput")
    yp_d = dram("yp", [SEQ, D], "ExternalOutput")
    ys_d = dram("ys", [SC, D], "ExternalOutput")
    mk_d = dram("mk", [DEPTH, NM, XAW], "ExternalOutput")
    mv_d = dram("mv", [DEPTH, NM, XAW], "ExternalOutput")
    cap_d = dram("cap", [NA, WA - 1, MIXW], "ExternalOutput")
    cbp_d = dram("cbp", [max(NB, 1), HPB, MIXW], "ExternalOutput")
    cas_d = dram("cas", [NA, SB * (WA - 1), MIXW], "ExternalOutput")
    cbs_d = dram("cbs", [max(NB, 1), SB * HPB, MIXW], "ExternalOutput")

    P = Prog()
    with ExitStack() as es:
        sbt = lambda n, s, d: es.enter_context(nc.sbuf_tensor(n, s, d))
        x = sbt("x", [128, KD, TN], F32)
        xn = sbt("xn", [128, KD, TN], BF16)
        br = sbt("br", [128, 16, TN], BF16)
        SCRW = max(KD * TN, 2 * UBW + TN + max(2 * TN, 2 * TPE_B * 64) + 2 * 512, 2 * D, 8448)
        scr = sbt("scr", [128, SCRW], F32)
        ring = sbt("ring", [128, NR, KD, 128], BF16)
        szt = [sbt("sz%d" % i, [128, 512], F32) for i in range(2)]
        sqt = [sbt("sq%d" % i, [128, 512], BF16) for i in range(2)]
        sq3 = sqt + [sbt("sq2", [128, 512], BF16)]
        szsA = sbt("szsA", [128, SC], F32)
        rstdt = [sbt("rstd%d" % i, [128, 512], F32) for i in range(2)]
        qTt = [sbt("qT%d" % i, [128, 512], BF16) for i in range(2)]
        pTt = [sbt("pT%d" % i, [128, 2, 512], BF16) for i in range(2)]
        att = [sbt("at%d" % i, [128, 512], F32) for i in range(2)]
        memhT = sbt("memhT", [128, KD, NM], F32)
        memn = sbt("memn", [128, KD, NM], BF16)
        kTp = sbt("kTp", [128, NH, NM], BF16)
        vp = sbt("vp", [128, 2, XAW], BF16)
        kvo = [sbt("kvo%d" % i, [128, XAW], F32) for i in range(2)]
        haloA = sbt("haloA", [128, NA, NJ, WA - 1], F32)
        haloB = sbt("haloB", [128, max(NB, 1), NJ, HPB], F32)
        ubbt = [sbt("ubb%d" % i, [128, UBW], BF16) for i in range(2)]
        vecs = sbt("vecs_sb", [128, NVEC], F32)
        idf = sbt("idf", [128, 128], F32)
        idb = sbt("idb", [128, 128], BF16)
        onesf = sbt("onesf", [128, 128], F32)
        onesb = sbt("onesb", [128, 128], BF16)
        Esel = sbt("Esel", [128, 8, 8], BF16)
        ps = [es.enter_context(nc.psum_tensor("ps%d" % i, [128, 512], F32)) for i in range(8)]

        ub = [scr[:, i * UBW:(i + 1) * UBW] for i in range(2)]
        o0 = 2 * UBW
        acc = scr[:, o0:o0 + TN]
        DG = o0 + TN
        DGW = max(2 * TN, 2 * TPE_B * 64)
        Abt = scr[:, DG:DG + TN]
        Btb = scr[:, DG + TN:DG + 2 * TN]
        diagt = [scr[:, DG + i * TPE_B * 64:DG + (i + 1) * TPE_B * 64].bitcast(BF16).rearrange("p (k m) -> p k m", m=128)
                 for i in range(2)] if TPE_B else [None, None]
        o1 = max(DG + DGW, 6400)
        thin = [scr[:, o1 + i * 512:o1 + (i + 1) * 512] for i in range(2)]
        outv = scr[:, 0:KD * TN].rearrange("p (o t) -> p o t", t=TN)
        NST = min(8, SCRW // D)
        stage = [scr[:, i * D:(i + 1) * D] for i in range(NST)]
        bfv = lambda w0, nw: scr[:, w0:w0 + nw].bitcast(BF16)
        NKV = 12
        kvstage = [bfv(i * 512, 512).rearrange("p (a e) -> p a e", e=XAW) for i in range(NKV)]
        kTs = [bfv(6144 + i * 512, 512).rearrange("p (h m) -> p h m", m=NM) for i in range(2)]
        pTs = bfv(7168, 512).rearrange("p (a e) -> p a e", e=512)
        qs = bfv(7680, 256).rearrange("p (h t) -> p h t", t=SC)
        szs = scr[:, 7936:8448].rearrange("p (h t) -> p h t", t=SC)
        sst = kvo
        sso = att
        stmp = rstdt[0]
        stsb = rstdt[1]
        qmask = [szt[0], szt[1]]

        cnt = {"bank": 0, "i": 0, "sq": 0}
        pend_ss = []
        dbg_out = {}

        def dbg(name, ap):
            if not DEBUG or name in dbg_out:
                return
            d = nc.dram_tensor("dbg_" + name, list(ap.shape), ap.dtype, kind="ExternalOutput").ap()
            dbg_out[name] = d
            P.op("sp", lambda e: e.dma_start(out=d, in_=ap), reads=[ap], dma=True)

        def V(off, n=1):
            return vecs[:, off:off + n]

        def mm(out, lhsT, rhs, start, stop, extra_r=(), extra_w=()):
            P.op("pe", lambda e: e.matmul(out, lhsT=lhsT, rhs=rhs, start=start, stop=stop),
                 reads=[lhsT, rhs], writes=[out])

        def tr(out, in_, ident):
            P.op("pe", lambda e: e.transpose(out=out, in_=in_, identity=ident), reads=[in_, ident], writes=[out])

        def act(out, in_, func, scale=1.0, bias=0.0):
            rd = [in_]
            if not isinstance(scale, (int, float)):
                rd.append(scale)
            if not isinstance(bias, (int, float)):
                rd.append(bias)
            P.op("act", lambda e: e.activation(out=out, in_=in_, func=func, bias=bias, scale=scale),
                 reads=rd, writes=[out])

        def tt(q, out, in0, in1, op):
            P.op(q, lambda e: e.tensor_tensor(out=out, in0=in0, in1=in1, op=op), reads=[in0, in1], writes=[out])

        def ts(q, out, in0, s1, op0, s2=None, op1=None):
            rd = [in0] + [s for s in (s1, s2) if s is not None and not isinstance(s, (int, float))]
            if op1 is None:
                P.op(q, lambda e: e.tensor_scalar(out=out, in0=in0, scalar1=s1, scalar2=None, op0=op0),
                     reads=rd, writes=[out])
            else:
                P.op(q, lambda e: e.tensor_scalar(out=out, in0=in0, scalar1=s1, scalar2=s2, op0=op0, op1=op1),
                     reads=rd, writes=[out])

        def stt(out, in0, scalar, in1, op0, op1):
            rd = [in0, in1] + ([] if isinstance(scalar, (int, float)) else [scalar])
            P.op("dve", lambda e: e.scalar_tensor_tensor(out=out, in0=in0, scalar=scalar, in1=in1, op0=op0, op1=op1),
                 reads=rd, writes=[out])

        def cp(q, out, in_):
            P.op(q, lambda e: e.tensor_copy(out=out, in_=in_), reads=[in_], writes=[out])

        def recip(out, in_):
            P.op("dve", lambda e: e.reciprocal(out=out, in_=in_), reads=[in_], writes=[out])

        def dma(q, out, in_):
            P.op(q, lambda e: e.dma_start(out=out, in_=in_), reads=[in_], writes=[out], dma=True)

        def memset(q, ap, val):
            P.op(q, lambda e: e.memset(ap, val), writes=[ap])

        def rsqrt_act(out, in_):
            act(out, in_, AF.Ln, bias=EPS)
            act(out, out, AF.Exp, scale=-0.5)

        memset("pool", onesf[:], 1.0)
        memset("pool", onesb[:], 1.0)
        P.op("pool", lambda e: e.affine_select(out=idf[:], in_=onesf[:], pattern=[[-1, 128]], compare_op=ALU.is_equal,
                                                fill=0.0, base=0, channel_multiplier=1),
             reads=[onesf[:]], writes=[idf[:]])
        cp("pool", idb[:], idf[:])
        memset("pool", Esel[:], 0.0)
        for r in range(8):
            memset("pool", Esel[:, r, r:r + 1], 1.0)
        dma("sp", vecs[:], vec_d[:])

        def load_transposed(dst3, src_rows, nrows, col0, ldq="sp"):
            i = cnt["i"]; cnt["i"] += 1
            stg = stage[i % NST]
            dma(ldq, stg[0:nrows, :], src_rows)
            for half in range(2):
                bank = ps[(2 * i + half) % 4]
                for k4 in range(4):
                    kc = half * 4 + k4
                    tr(bank[:, k4 * 128:k4 * 128 + nrows], stg[0:nrows, kc * 128:(kc + 1) * 128], idf[0:nrows, 0:nrows])
                src = bank[:, :].rearrange("p (k t) -> p k t", t=128)[:, :, 0:nrows]
                act(dst3[:, half * 4:half * 4 + 4, col0:col0 + nrows], src, AF.Copy)

        def store_transposed(dst_rows, src3, nrows, col0):
            i = cnt["i"]; cnt["i"] += 1
            stg = stage[i % NST]
            for half in range(2):
                bank = ps[(2 * i + half) % 4]
                for k4 in range(4):
                    kc = half * 4 + k4
                    tr(bank[0:nrows, k4 * 128:(k4 + 1) * 128], src3[:, kc, col0:col0 + nrows], idf[:, :])
                act(stg[0:nrows, half * 512:(half + 1) * 512], bank[0:nrows, :], AF.Copy)
            dma("sp", dst_rows, stg[0:nrows, :])

        for mc in range(2):
            load_transposed(memhT, mem_d[mc * 128:(mc + 1) * 128, :], 128, mc * 128)
        for kc in range(KD):
            act(sqt[kc % 2][:, 0:NM], memhT[:, kc, :], AF.Square, scale=1.0 / 32.0)
            mm(ps[4][:, 0:NM], onesb[:], sqt[kc % 2][:, 0:NM], kc == 0, kc == KD - 1)
        rsqrt_act(rstdt[0][:, 0:NM], ps[4][:, 0:NM])
        tt("dve", memhT[:], memhT[:], rstdt[0][:, 0:NM].unsqueeze(1).broadcast_to([128, KD, NM]), ALU.mult)

        def emit_memn(l):
            off = (V_A_MEM if l % 2 == 0 else V_B_MEM) + (l // 2) * 8
            tt("dve", memn[:], memhT[:], V(off, 8).unsqueeze(2).broadcast_to([128, KD, NM]), ALU.mult)

        pending = {}
        items = []

        def wsrc(w_d, l, r0, c0):
            return w_d[l, r0:r0 + 1024, c0:c0 + 128].rearrange("(kc p) m -> p kc m", p=128)

        for pz in range(NPASS):
            has_s = (pz == 0)
            ptl = [("p", i * PT, PT) for i in range(NPT)]
            stl = [("s", PC, SC)] if has_s else []
            ctilesA = ptl[:1] + stl + ptl[1:]
            ctilesB = stl + ptl
            ctiles = ctilesA
            ncols = PC + (SC if has_s else 0)
            last_pass = (pz == NPASS - 1)
            tok0 = pz * PC

            def prologue(pz=pz, has_s=has_s, tok0=tok0):
                for tb in range(PC // 128) if PC >= 128 else []:
                    load_transposed(x, xp_d[tok0 + tb * 128:tok0 + (tb + 1) * 128, :], 128, tb * 128,
                                    ldq=("pool" if pz > 0 else "sp"))
                if has_s:
                    load_transposed(x, xs_d[:, :], 128, PC)
            items.append(([], prologue))

            def emit_layer(l, pz=pz, has_s=has_s, last_pass=last_pass, ctilesA=ctilesA, ctilesB=ctilesB):
                ctiles = ctilesA if l % 2 == 0 else ctilesB
                isA = (l % 2 == 0)
                jl = l // 2
                win_d, wkv_d, wout_d = (awin_d, awkv_d, awout_d) if isA else (bwin_d, bwkv_d, bwout_d)
                vpre, vpost, vmem = (V_A_PRE, V_A_POST, V_A_MEM) if isA else (V_B_PRE, V_B_POST, V_B_MEM)
                W = WA if isA else WB
                HP = W - 1
                SW = HP + ST
                SOFF = HP + PC
                if isA:
                    c_h, c_b, c_c, c_z, c_q = 0, MIXW, 2 * MIXW, 3 * MIXW, 3 * MIXW + BRW
                else:
                    c_v, c_g, c_z, c_q = 0, MIXW, 2 * MIXW, 2 * MIXW + BRW
                cw_off = (V_A_CW + jl * NJ * WA) if isA else (V_B_CW + jl * NJ * WB)

                def stepM_pre():
                    if (pz == 0 and l == 0) or not PREMEM:
                        emit_memn(l)

                def stepM_k(slots, e, l=l, pz=pz):
                    w = slots[0]
                    bank = ps[e % 2]
                    for kc in range(KD):
                        mm(bank[:, 0:NM], w[:, kc, :], memn[:, kc, :], kc == 0, kc == KD - 1)
                    act(kTp[:, e, :], bank[:, 0:NM], AF.Copy)
                    if pz == 0:
                        for mc in range(2):
                            for kc in range(KD):
                                mm(ps[2 + mc][:, e * 128:(e + 1) * 128], memn[:, kc, mc * 128:(mc + 1) * 128], w[:, kc, :],
                                   kc == 0, kc == KD - 1)
                        if e == NH - 1:
                            for mc in range(2):
                                act(kvo[mc][:], ps[2 + mc][:], AF.Copy)
                                dma("sp", mk_d[l, mc * 128:(mc + 1) * 128, :], kvo[mc][:])

                def stepM_v(slots, e, l=l, pz=pz):
                    w = slots[0]
                    for mc in range(2):
                        for kc in range(KD):
                            mm(ps[2 + mc][:, e * 128:(e + 1) * 128], memn[:, kc, mc * 128:(mc + 1) * 128], w[:, kc, :],
                               kc == 0, kc == KD - 1)
                    if e == NH - 1:
                        for mc in range(2):
                            act(vp[:, mc, :], ps[2 + mc][:], AF.Copy)
                            if pz == 0:
                                act(kvo[mc][:], ps[2 + mc][:], AF.Copy)
                                dma("sp", mv_d[l, mc * 128:(mc + 1) * 128, :], kvo[mc][:])

                items.append(([], stepM_pre))
                for e in range(NH):
                    items.append(([("ring", wsrc(wkv_d, jl, 0, e * 128))], lambda s, e=e, f=stepM_k: f(s, e)))
                mv_idx = []
                for e in range(NH):
                    mv_idx.append(len(items))
                    items.append(([("ring", wsrc(wkv_d, jl, 0, XAW + e * 128))], lambda s, e=e, f=stepM_v: f(s, e)))

                post_prev = pending.pop("post", None)

                def stepN(only=None, jl=jl, vpre=vpre, ctiles=ctiles, HP=HP, SOFF=SOFF, pz=pz):
                    for ci, (kind, c0, w) in enumerate(ctiles):
                        if only is not None and ci != only:
                            continue
                        if post_prev is not None:
                            post_prev(c0)
                        bank = ps[4 + ci % 2]
                        for kc in range(KD):
                            i = cnt["i"]; cnt["i"] += 1
                            act(sqt[i % 2][:, 0:w], x[:, kc, c0:c0 + w], AF.Square, scale=1.0 / 32.0)
                            mm(bank[:, 0:w], onesb[:], sqt[i % 2][:, 0:w], kc == 0, kc == KD - 1)
                        rs = rstdt[ci % 2]
                        rsqrt_act(rs[:, 0:w], bank[:, 0:w])
                        for kc in range(KD):
                            stt(xn[:, kc, c0:c0 + w], x[:, kc, c0:c0 + w], V(vpre + jl * 8 + kc), rs[:, 0:w],
                                ALU.mult, ALU.mult)
                    if only is not None and only != len(ctiles) - 1:
                        return
                    if pz == 0:
                        for b2 in range(2):
                            memset("pool", ub[b2][:, 0:HP], 0.0)
                            if vpre == V_B_PRE and TPE_B:
                                memset("pool", ubbt[b2][:, 0:HP], 0.0)
                    dbg("xn_l%d_p%d" % (jl * 2 + (0 if vpre == V_A_PRE else 1), pz), xn[:])
                    dbg("x_l%d_p%d" % (jl * 2 + (0 if vpre == V_A_PRE else 1), pz), x[:])
                nt_ = len(ctiles)
                for k_ in range(1, nt_):
                    items.append(([], (lambda k_=k_: stepN(k_))))
                items.insert(mv_idx[0] + 1, ([], (lambda: stepN(0))))

                TPE = 0 if isA else TPE_B
                rotl = [5, 6, 7] if has_s else [4, 5, 6, 7]

                def rot_bank():
                    r = cnt["bank"]; cnt["bank"] += 1
                    return ps[rotl[r % len(rotl)]]
                hb_out = ps[3]
                halo_l = haloA[:, jl] if isA else haloB[:, jl]

                def ub_views(ubuf, kind, c0, w):
                    if kind == "p":
                        return ubuf[:, HP + c0:HP + c0 + w], (lambda k: ubuf[:, c0 + k:c0 + k + w])
                    us = ubuf[:, SOFF:SOFF + SB * SW].rearrange("p (s b) -> p b s", b=SB)
                    return us[:, :, HP:HP + ST], (lambda k: us[:, :, k:k + ST])

                def tview(ap2, kind, w):
                    return ap2 if kind == "p" else ap2.rearrange("p (b t) -> p b t", t=ST)

                sring = [kvo[0], kvo[1], rstdt[1]] if SRING3 else [kvo[0], kvo[1], kvo[0]]

                def head_dma(j):
                    if has_s:
                        st = sring[j % 3]
                        if isA:
                            nr = SB * HP
                            dma("sp", st[0:nr, 0:128], sa_d[jl, :, j * 128:(j + 1) * 128])
                        else:
                            nr = 4 * HP
                            dma("sp", st[0:nr, :].rearrange("r (k c) -> r k c", c=128),
                                sb_d[jl, :, j * 128:(j + 1) * 128].rearrange("(k r) c -> r k c", r=nr))

                def diag_build(j):
                    if TPE:
                        wv = vecs[:, cw_off + j * W:cw_off + j * W + TPE]
                        tt(DIAGQ, diagt[j % 2], idb[:].unsqueeze(1).broadcast_to([128, TPE, 128]),
                           wv.unsqueeze(2).broadcast_to([128, TPE, 128]), ALU.mult)

                def head_rest(j):
                    ubuf = ub[j % 2]
                    bufs = [ubuf] + ([ubbt[j % 2]] if TPE else [])
                    if pz > 0:
                        for bf in bufs:
                            act(bf[:, 0:HP], halo_l[:, j, 0:HP], AF.Copy)
                    if has_s:
                        st = sring[j % 3]
                        hb_in = rot_bank() if isA else ps[HBB]
                        if isA:
                            nr = SB * HP
                            tr(hb_in[:, 0:nr], st[0:nr, 0:128], idf[0:nr, 0:nr])
                        else:
                            nr = 4 * HP
                            for k in range(4):
                                tr(hb_in[:, k * nr:(k + 1) * nr], st[0:nr, k * 128:(k + 1) * 128], idf[0:nr, 0:nr])
                        for bf in bufs:
                            us = bf[:, SOFF:SOFF + SB * SW].rearrange("p (s b) -> p b s", b=SB)
                            act(us[:, :, 0:HP], hb_in[:, 0:SB * HP].rearrange("p (b r) -> p b r", r=HP), AF.Copy)

                def tail_a(j):
                    ubuf = ub[j % 2]
                    act(halo_l[:, j, 0:HP], ubuf[:, PC:PC + HP], AF.Copy)
                    if has_s:
                        us = ubuf[:, SOFF:SOFF + SB * SW].rearrange("p (s b) -> p b s", b=SB)
                        act(stmp[:, 0:SB * HP].rearrange("p (b r) -> p b r", r=HP), us[:, :, ST:ST + HP], AF.Copy)

                def tail_b(j):
                    cd, csd = (cap_d, cas_d) if isA else (cbp_d, cbs_d)
                    hb_out = rot_bank() if isA else ps[3]
                    if last_pass:
                        i = cnt["i"]; cnt["i"] += 1
                        so = sso[i % 2]
                        tr(hb_out[0:HP, 0:128], halo_l[:, j, 0:HP], idf[:, :])
                        act(so[0:HP, 0:128], hb_out[0:HP, 0:128], AF.Copy)
                        dma("sp", cd[jl, :, j * 128:(j + 1) * 128], so[0:HP, 0:128])
                    if has_s:
                        i = cnt["i"]; cnt["i"] += 1
                        so = sso[i % 2]
                        if isA:
                            nr = SB * HP
                            tr(hb_out[0:nr, 0:128], stmp[:, 0:nr], idf[:, :])
                            act(so[0:nr, 0:128], hb_out[0:nr, 0:128], AF.Copy)
                            dma("sp", csd[jl, :, j * 128:(j + 1) * 128], so[0:nr, 0:128])
                        else:
                            nr = 4 * HP
                            for k in range(4):
                                tr(hb_out[0:nr, k * 128:(k + 1) * 128], stmp[:, k * nr:(k + 1) * nr], idf[:, :])
                            act(so[0:nr, :], hb_out[0:nr, :], AF.Copy)
                            dma("sp", csd[jl, :, j * 128:(j + 1) * 128].rearrange("(k r) c -> r k c", r=nr),
                                so[0:nr, :].rearrange("r (k c) -> r k c", c=128))

                def conv_taps(ubuf, j, tiles, k0):
                    vs = []
                    for (kind, c0, w) in tiles:
                        if kind == "p":
                            hw_ = max(w // 2, 1)
                            for s0 in range(0, w, hw_):
                                vs.append((acc[:, c0 + s0:c0 + s0 + hw_],
                                           (lambda k, a0=c0 + s0, ww=hw_: ubuf[:, a0 + k:a0 + k + ww])))
                        else:
                            _, rd = ub_views(ubuf, kind, c0, w)
                            vs.append((tview(acc[:, c0:c0 + w], kind, w), rd))
                    for k in range(k0, W):
                        wk = V(cw_off + j * W + k)
                        for a_, rd in vs:
                            if k == k0:
                                ts("dve", a_, rd(k), wk, ALU.mult)
                            else:
                                stt(a_, rd(k), wk, a_, ALU.mult, ALU.add)

                def pe_taps(j, tiles):
                    ubb = ubbt[j % 2]
                    for ci, (kind, c0, w) in enumerate(tiles):
                        if kind == "p":
                            rd = (lambda k, c0=c0, w=w: ubb[:, c0 + k:c0 + k + w])
                        else:
                            rd = (lambda k: ubb[:, SOFF + k * SB:SOFF + (k + ST) * SB])
                        for k in range(TPE):
                            mm(ps[4 + ci][:, 0:w], diagt[j % 2][:, k, :], rd(k), k == 0, k == TPE - 1)

                def item_pre(j):
                    if j == 0:
                        head_dma(0)
                        diag_build(0)
                        head_rest(0)
                    if j + 1 < NJ:
                        head_dma(j + 1)
                        if HEADPOS < 0:
                            head_rest(j + 1)

                def item_tile(j, ti):
                    hp_ = HEADPOS if isA else len(ctiles) - 1
                    if HEADPOS >= 0 and ti == min(hp_, len(ctiles) - 1) and j + 1 < NJ:
                        head_rest(j + 1)

                def item_mid(j):
                    if j + 1 < NJ:
                        diag_build(j + 1)
                    if j > 0:
                        tail_b(j - 1)

                def item_post(j):
                    tail_a(j)
                    if j == NJ - 1:
                        tail_b(j)

                if isA:
                    def p1A(slots, j):
                        wh, wb_, wc, wz = slots
                        ubuf = ub[j % 2]
                        item_pre(j)
                        post = []
                        for ti, (kind, c0, w) in enumerate(ctiles):
                            i = cnt["i"]; cnt["i"] += 1
                            if kind == "p":
                                bh, bc = rot_bank()[:, 0:w], rot_bank()[:, 0:w]
                                pi_ = c0 // PT
                                bb, bz = ps[2 * pi_][:, 0:w], ps[2 * pi_ + 1][:, 0:w]
                            else:
                                bh, bc, bz = rot_bank()[:, 0:w], rot_bank()[:, 0:w], rot_bank()[:, 0:w]
                                bb = ps[4][:, 0:w]
                            for wt_, bk in ((wh, bh), (wc, bc), (wb_, bb), (wz, bz)):
                                for kc in range(KD):
                                    mm(bk, wt_[:, kc, :], xn[:, kc, c0:c0 + w], kc == 0, kc == KD - 1)
                            th = thin[i % 2]
                            act(th[:, 0:w], bh, AF.Copy)
                            uw, _ = ub_views(ubuf, kind, c0, w)
                            tt("dve", uw, tview(bc, kind, w), tview(th[:, 0:w], kind, w), ALU.mult)
                            if kind != "p":
                                act(szsA[:, 0:w], bz, AF.Silu)
                                bz = None
                            post.append((c0, w, bb, bz))
                            item_tile(j, ti)
                        item_mid(j)
                        conv_taps(ubuf, j, ctiles, 0)
                        for (c0, w, bb, bz) in post:
                            tt("dve", acc[:, c0:c0 + w], acc[:, c0:c0 + w], bb, ALU.mult)
                        for (c0, w, bb, bz) in post:
                            i = cnt["i"]; cnt["i"] += 1
                            sz = szt[i % 2]
                            if bz is None:
                                sz = szsA
                            else:
                                act(sz[:, 0:w], bz, AF.Silu)
                            tt("dve", br[:, j, c0:c0 + w], acc[:, c0:c0 + w], sz[:, 0:w], ALU.mult)
                        if j == 0:
                            dbg("ubA_l%d_p%d" % (l, pz), ubuf)
                            dbg("accA_l%d_p%d" % (l, pz), acc)
                        item_post(j)
                    for j in range(NJ):
                        items.append(([("ring", wsrc(win_d, jl, 0, c + j * 128)) for c in (c_h, c_b, c_c, c_z)],
                                      lambda s, j=j, f=p1A: f(s, j)))
                else:
                    def ln_stat_mm(j):
                        order = sorted(range(len(ctiles)), key=lambda c_: -ctiles[c_][2])
                        for n_, ci in enumerate(order):
                            kind, c0, w = ctiles[ci]
                            first = (j == 0 and n_ == 0)
                            last = (j == NJ - 1 and n_ == len(order) - 1)
                            mm(ps[7][0:8, 0:w], Esel[:, ci, :], br[:, j, c0:c0 + w], first, False)
                            mm(ps[7][0:8, 0:w], Esel[:, 4 + ci, :], sq3[ci][:, 0:w], False, last)

                    def post_act(j):
                        for ci, (kind, c0, w) in enumerate(ctiles):
                            cb = V(V_B_CB + jl * NJ + j)
                            act(br[:, j, c0:c0 + w], acc[:, c0:c0 + w], AF.Identity, bias=cb)
                            act(sq3[ci][:, 0:w], acc[:, c0:c0 + w], AF.Square, bias=cb)

                    def p1B(slots, j):
                        wv, wg = slots
                        ubuf = ub[j % 2]
                        item_pre(j)
                        for ti, (kind, c0, w) in enumerate(ctiles):
                            i = cnt["i"]; cnt["i"] += 1
                            b0 = 2 * (ti % 2)
                            bv, bg = ps[b0], ps[b0 + 1]
                            for wt_, bk in ((wg, bg), (wv, bv)):
                                for kc in range(KD):
                                    mm(bk[:, 0:w], wt_[:, kc, :], xn[:, kc, c0:c0 + w], kc == 0, kc == KD - 1)
                            th = thin[i % 2]
                            act(th[:, 0:w], bg[:, 0:w], AF.Sigmoid)
                            uw, _ = ub_views(ubuf, kind, c0, w)
                            tt("dve", uw, tview(bv[:, 0:w], kind, w), tview(th[:, 0:w], kind, w), ALU.mult)
                            if TPE:
                                uwb, _ = ub_views(ubbt[j % 2], kind, c0, w)
                                act(uwb, uw, AF.Copy)
                            if ti == min(1, len(ctiles) - 1) and j > 0 and DEFER:
                                post_act(j - 1)
                            item_tile(j, ti)
                        item_mid(j)
                        if TPE:
                            pe_taps(j, ctiles)
                        if j > 0 and DEFER:
                            ln_stat_mm(j - 1)
                        conv_taps(ubuf, j, ctiles, TPE)
                        if TPE:
                            for ci, (kind, c0, w) in enumerate(ctiles):
                                if kind == "p":
                                    tt("dve", acc[:, c0:c0 + w], acc[:, c0:c0 + w], ps[4 + ci][:, 0:w], ALU.add)
                                else:
                                    a3 = acc[:, c0:c0 + w].rearrange("p (b t) -> p b t", t=ST)
                                    tt("dve", a3, a3, ps[4 + ci][:, 0:w].rearrange("p (t b) -> p b t", b=SB), ALU.add)
                        item_post(j)
                        if j == NJ - 1 or not DEFER:
                            post_act(j)
                            ln_stat_mm(j)
                    for j in range(NJ):
                        items.append(([("ring", wsrc(win_d, jl, 0, c + j * 128)) for c in (c_v, c_g)],
                                      lambda s, j=j, f=p1B: f(s, j)))

                    def lnstats(ctiles=ctiles):
                        act(stsb[0:8, :], ps[7][0:8, :], AF.Copy)
                        for ci, (kind, c0, w) in enumerate(ctiles):
                            for si, r in enumerate((ci, 4 + ci)):
                                ts("dve", qmask[si][0:8, 0:w], stsb[0:8, 0:w], idf[0:8, r:r + 1], ALU.mult)
                                mm(ps[4 + si][:, 0:w], onesf[0:8, :], qmask[si][0:8, 0:w], True, True)
                            mean = att[0]
                            ts("dve", mean[:, 0:w], ps[4][:, 0:w], 1.0 / MIXW, ALU.mult)
                            msq = att[1]
                            tt("dve", msq[:, 0:w], mean[:, 0:w], mean[:, 0:w], ALU.mult)
                            stt(msq[:, 0:w], ps[5][:, 0:w], 1.0 / MIXW, msq[:, 0:w], ALU.mult, ALU.subtract)
                            rsqrt_act(Abt[:, c0:c0 + w], msq[:, 0:w])
                            stt(Btb[:, c0:c0 + w], mean[:, 0:w], -1.0, Abt[:, c0:c0 + w], ALU.mult, ALU.mult)
                    items.append(([], lnstats))

                    def p2B(slots, j, ctiles=ctiles, jl=jl):
                        wz = slots[0]
                        szb = [szt[0], szt[1], rstdt[1]]
                        t1b = [att[0], att[1], rstdt[0]]
                        for ti, (kind, c0, w) in enumerate(ctiles):
                            i = cnt["i"]; cnt["i"] += 1
                            bz = ps[i % 4]
                            for kc in range(KD):
                                mm(bz[:, 0:w], wz[:, kc, :], xn[:, kc, c0:c0 + w], kc == 0, kc == KD - 1)
                            act(szb[ti][:, 0:w], bz[:, 0:w], AF.Silu)
                            tt("dve", t1b[ti][:, 0:w], br[:, j, c0:c0 + w], Abt[:, c0:c0 + w], ALU.mult)
                        for ti, (kind, c0, w) in enumerate(ctiles):
                            tt("dve", t1b[ti][:, 0:w], t1b[ti][:, 0:w], Btb[:, c0:c0 + w], ALU.add)
                        for ti, (kind, c0, w) in enumerate(ctiles):
                            act(t1b[ti][:, 0:w], t1b[ti][:, 0:w], AF.Silu, scale=V(V_B_LG + jl * NJ + j),
                                bias=V(V_B_LB + jl * NJ + j))
                        for ti, (kind, c0, w) in enumerate(ctiles):
                            tt("dve", br[:, j, c0:c0 + w], t1b[ti][:, 0:w], szb[ti][:, 0:w], ALU.mult)
                    for j in range(NJ):
                        items.append(([("ring", wsrc(win_d, jl, 0, c_z + j * 128))], lambda s, j=j, f=p2B: f(s, j)))

                def p3(slots, h, ctiles=ctiles):
                    wq, wz = slots
                    scale = float(128 ** -0.5)
                    A0 = 0 if h % 2 == 0 else 4
                    B0 = 4 - A0
                    pts = [t for t in ctiles if t[0] == "p"]
                    for ti, (kind, c0, w) in enumerate(ctiles):
                        bq, bz = ps[A0 + 2 * (ti % 2)][:, 0:w], ps[A0 + 2 * (ti % 2) + 1][:, 0:w]
                        for wt_, bk in ((wq, bq), (wz, bz)):
                            for kc in range(KD):
                                mm(bk, wt_[:, kc, :], xn[:, kc, c0:c0 + w], kc == 0, kc == KD - 1)
                        if kind == "p":
                            pi_ = (c0 // PT) % 2
                            qT, sz, zc = qTt[pi_][:, 0:w], szt[pi_][:, 0:w], rstdt[pi_][:, 0:w]
                        else:
                            qT, sz, zc = qs[:, h, :], szs[:, h, :], szsA[:, 0:w]
                        cp("dve", qT, bq)
                        act(sz, bz, AF.Exp, scale=-1.0)
                        act(sz, sz, AF.Ln, bias=1.0)
                        act(sz, sz, AF.Exp, scale=-1.0)
                        tt("dve", sz, sz, bz, ALU.mult)
                    for n_, (kind, c0, w) in enumerate(pts):
                        pi_ = (c0 // PT) % 2
                        for mc in range(2):
                            bk = ps[B0 + 2 * (n_ % 2) + mc]
                            mm(bk[:, 0:w], kTp[:, h, mc * 128:(mc + 1) * 128], qTt[pi_][:, 0:w], True, True)
                            act(pTt[pi_][:, mc, 0:w], bk[:, 0:w], AF.Exp, scale=scale)
                    for n_, (kind, c0, w) in enumerate(pts):
                        pi_ = (c0 // PT) % 2
                        pT = pTt[pi_]
                        bd, bo_ = ps[A0 + 2 * (n_ % 2)], ps[A0 + 2 * (n_ % 2) + 1]
                        for mc in range(2):
                            mm(bd[:, 0:w], onesb[:], pT[:, mc, 0:w], mc == 0, mc == 1)
                        for mc in range(2):
                            mm(bo_[:, 0:w], vp[:, mc, h * 128:(h + 1) * 128], pT[:, mc, 0:w], mc == 0, mc == 1)
                        rd = att[pi_]
                        act(rd[:, 0:w], bd[:, 0:w], AF.Ln)
                        act(rd[:, 0:w], rd[:, 0:w], AF.Exp, scale=-1.0)
                        tt("dve", rd[:, 0:w], rd[:, 0:w], szt[pi_][:, 0:w], ALU.mult)
                        tt("dve", br[:, NJ + h, c0:c0 + w], bo_[:, 0:w], rd[:, 0:w], ALU.mult)
                for h in range(NH):
                    items.append(([("ring", wsrc(win_d, jl, 0, c_q + h * 128)),
                                   ("ring", wsrc(win_d, jl, 0, c_z + (NJ + h) * 128))],
                                  lambda s, h=h, f=p3: f(s, h), "p3"))

                if has_s:
                    scale = float(128 ** -0.5)

                    def samp_scores(b):
                        kt = kTs[b % 2]
                        for h in range(NH):
                            for mc in range(2):
                                col = (b * NH + h) * ST
                                mm(ps[2 + mc][:, col:col + ST], kt[:, h, mc * 128:(mc + 1) * 128],
                                   qs[:, h, b * ST:(b + 1) * ST], True, True)

                    def samp_k(slots, b):
                        kr = slots[0]
                        bankb = ps[b % 2][:].bitcast(BF16)
                        for h in range(NH):
                            for mc in range(2):
                                tr(bankb[:, (h * 2 + mc) * 128:(h * 2 + mc + 1) * 128], kr[:, mc, h * 128:(h + 1) * 128], idb[:])
                        act(kTs[b % 2].rearrange("p h m -> p (h m)"), bankb[:, :], AF.Copy)
                        if b > 0:
                            samp_scores(b - 1)
                        if b == SB - 1:
                            samp_scores(b)
                            for mc in range(2):
                                act(pTs[:, mc, :], ps[2 + mc][:, :], AF.Exp, scale=scale)
                            for mc in range(2):
                                mm(ps[4][:, :], onesb[:], pTs[:, mc, :], mc == 0, mc == 1)

                    def samp_v(slots, b):
                        vr = slots[0]
                        for h in range(NH):
                            col = (b * NH + h) * ST
                            for mc in range(2):
                                mm(ps[5][:, col:col + ST], vr[:, mc, h * 128:(h + 1) * 128], pTs[:, mc, col:col + ST],
                                   mc == 0, mc == 1)
                        if b == SB - 1:
                            rd = att[0]
                            act(rd[:], ps[4][:], AF.Ln)
                            act(rd[:], rd[:], AF.Exp, scale=-1.0)
                            for h in range(NH):
                                rv = rd[:].rearrange("p (b h t) -> p b h t", h=NH, t=ST)[:, :, h, :]
                                ov = ps[5][:].rearrange("p (b h t) -> p b h t", h=NH, t=ST)[:, :, h, :]
                                t2 = att[1][:, 0:SC].rearrange("p (b t) -> p b t", t=ST)
                                tt("dve", t2, rv, szs[:, h, :].rearrange("p (b t) -> p b t", t=ST), ALU.mult)
                                tt("dve", br[:, NJ + h, PC:PC + SC].rearrange("p (b t) -> p b t", t=ST), ov, t2, ALU.mult)
                    for b in range(SB):
                        items.append(([("buf", None, ck_d[l, b].rearrange("(mc p) e -> p mc e", p=128))],
                                      lambda s, b=b, f=samp_k: f(s, b), "p3"))
                    for b in range(SB):
                        items.append(([("buf", None, cv_d[l, b].rearrange("(mc p) e -> p mc e", p=128))],
                                      lambda s, b=b, f=samp_v: f(s, b), "p3"))

                def stepO(slots, o, ctiles=ctiles, jl=jl, vpost=vpost):
                    w0, w1 = slots
                    if o == 0 and PREMEM:
                        emit_memn((l + 1) % DEPTH)
                    for ci, (kind, c0, w) in enumerate(ctiles):
                        i = cnt["i"]; cnt["i"] += 1
                        bo = ps[i % 4]
                        for kc in range(16):
                            wt_ = w0 if kc < 8 else w1
                            mm(bo[:, 0:w], wt_[:, kc % 8, :], br[:, kc, c0:c0 + w], kc == 0, kc == 15)
                        if pend_ss:
                            pend_ss.pop(0)()
                        act(outv[:, o, c0:c0 + w], bo[:, 0:w], AF.Identity, scale=V(vpost + jl * 8 + o))
                        k_ = cnt["sq"]; cnt["sq"] += 1
                        sq = sq3[k_ % 3]
                        act(sq[:, 0:w], bo[:, 0:w], AF.Square, scale=1.0 / 32.0)
                        pend_ss.append(lambda ci=ci, w=w, sq=sq, o=o: mm(ps[5 + ci][:, 0:w], onesb[:], sq[:, 0:w],
                                                                       o == 0, o == KD - 1))
                    if o == KD - 1:
                        while pend_ss:
                            pend_ss.pop(0)()

                def stepO_post(only=None):
                    for ci, (kind, c0, w) in enumerate(ctiles):
                        if only is not None and c0 != only:
                            continue
                        rs = rstdt[ci % 2]
                        rsqrt_act(rs[:, 0:w], ps[5 + ci][:, 0:w])
                        for oo in range(KD):
                            q_ = "dve"
                            tt(q_, outv[:, oo, c0:c0 + w], outv[:, oo, c0:c0 + w], rs[:, 0:w], ALU.mult)
                            tt(q_, x[:, oo, c0:c0 + w], x[:, oo, c0:c0 + w], outv[:, oo, c0:c0 + w], ALU.add)
                pending["post"] = stepO_post
                items.append(([], lambda l=l, pz=pz: dbg("br_l%d_p%d" % (l, pz), br[:])))
                for o in range(KD):
                    items.append(([("ring", wsrc(wout_d, jl, 0, o * 128)), ("ring", wsrc(wout_d, jl, 1024, o * 128))],
                                  lambda s, o=o, f=stepO: f(s, o)))

            for l in range(DEPTH):
                emit_layer(l)
            items.append(([], pending.pop("post")))

            def epilogue(pz=pz, has_s=has_s, tok0=tok0):
                for tb in range(PC // 128):
                    store_transposed(yp_d[tok0 + tb * 128:tok0 + (tb + 1) * 128, :], x, 128, tb * 128)
                if has_s:
                    store_transposed(ys_d[:, :], x, 128, PC)
            items.append(([], epilogue))

        rstate = {"next": 0, "kv": 0}
        issued = [None] * len(items)

        def issue(idx):
            loads = items[idx][0]
            slots = []
            for ld in loads:
                if ld[0] == "ring":
                    s = rstate["next"] % NR
                    rstate["next"] += 1
                    dst = ring[:, s]
                    dma("pool", dst, ld[1])
                    slots.append(dst)
                else:
                    dst = kvstage[rstate["kv"] % NKV]
                    rstate["kv"] += 1
                    dma("pool", dst, ld[2])
                    slots.append(dst)
            issued[idx] = slots

        def nring(idx):
            return sum(1 for ld in items[idx][0] if ld[0] == "ring")

        nxt = 0
        live = []
        for i in range(len(items)):
            while nxt < len(items):
                if nxt <= i:
                    pass
                else:
                    units = sum(u for _, u in live) + nring(nxt)
                    nbuf = sum(1 for k in range(i, nxt) if any(ld[0] == "buf" for ld in items[k][0]))
                    isbuf = any(ld[0] == "buf" for ld in items[nxt][0])
                    if units > NR or nxt - i > 24 or (isbuf and (nbuf >= NKV or len(items[i]) < 3)):
                        break
                issue(nxt)
                live.append((nxt, nring(nxt)))
                nxt += 1
            loads, fn = items[i][0], items[i][1]
            if loads:
                fn(issued[i])
            else:
                fn()
            live = [(k, u) for (k, u) in live if k != i]

        P.finish()
        P.emit(nc, es)
    return nc


def _pack_vecs(inp, DEPTH):
    NA = (DEPTH + 1) // 2
    NB = DEPTH // 2
    v = np.zeros((128, NVEC), np.float32)

    def put8(off, arr, n):
        for l in range(n):
            v[:, off + l * 8:off + (l + 1) * 8] = arr[l].reshape(8, 128).T
    put8(V_A_PRE, inp["a_norm_pre"], NA)
    put8(V_A_POST, inp["a_norm_post"], NA)
    put8(V_A_MEM, inp["a_mem_norm"], NA)
    put8(V_B_PRE, inp["b_norm_pre"], NB)
    put8(V_B_POST, inp["b_norm_post"], NB)
    put8(V_B_MEM, inp["b_mem_norm"], NB)
    for l in range(NA):
        w = inp["a_conv_w"][l]
        v[:, V_A_CW + l * NJ * WA:V_A_CW + (l + 1) * NJ * WA] = \
            w.reshape(WA, NJ, 128).transpose(2, 1, 0).reshape(128, NJ * WA)
    for l in range(NB):
        w = inp["b_conv_w"][l]
        v[:, V_B_CW + l * NJ * WB:V_B_CW + (l + 1) * NJ * WB] = \
            w.reshape(WB, NJ, 128).transpose(2, 1, 0).reshape(128, NJ * WB)
        for off, key in ((V_B_CB, "b_conv_b"), (V_B_LG, "b_ln_g"), (V_B_LB, "b_ln_b")):
            v[:, off + l * NJ:off + (l + 1) * NJ] = inp[key][l].reshape(NJ, 128).T
    return v


_CACHE = {}
DEBUG = False
BUILD_KW = {}
DBG_RES = {}


def kernel(**inp):
    inp = {k: np.asarray(v) for k, v in inp.items()}
    NB_, SEQ, _ = inp["x_prompt"].shape
    DEPTH = inp["cache_mem_k"].shape[0]
    ncores = NB_
    key = (SEQ, DEPTH)
    if key not in _CACHE:
        _CACHE[key] = build_program(SEQ=SEQ, DEPTH=DEPTH, **BUILD_KW)
    nc = _CACHE[key]
    NA = (DEPTH + 1) // 2
    NB = DEPTH // 2
    vecs = _pack_vecs(inp, DEPTH)
    f = lambda a: np.ascontiguousarray(a, dtype=np.float32)
    shared = {k: f(inp[k]) for k in ("a_w_in", "a_w_kv", "a_w_out", "b_w_in", "b_w_kv", "b_w_out")}
    in_maps = []
    for c in range(ncores):
        sl = slice(c * SB, (c + 1) * SB)
        m = dict(shared)
        m["xp"] = f(inp["x_prompt"][c])
        m["xs"] = f(inp["x_sample"][sl].reshape(SC, D))
        m["mem"] = f(inp["mem_prompt"][c])
        m["ck"] = f(inp["cache_mem_k"][:, sl].reshape(DEPTH, SB, NM, XAW))
        m["cv"] = f(inp["cache_mem_v"][:, sl].reshape(DEPTH, SB, NM, XAW))
        m["sa"] = f(inp["state_conv_a"][:, sl].reshape(NA, SB * (WA - 1), MIXW))
        m["sb"] = f(inp["state_conv_b"][:, sl].reshape(NB, SB * (WB - 1), MIXW))
        m["vecs"] = vecs
        in_maps.append(m)
    res = run_bass_kernel_spmd(nc, in_maps, core_ids=list(range(ncores)))
    R = res.results
    if DEBUG:
        DBG_RES.update({k: v for k, v in R[0].items() if k.startswith("dbg_")})
    y_p = np.stack([R[c]["yp"] for c in range(ncores)])
    y_s = np.concatenate([R[c]["ys"].reshape(SB, ST, D) for c in range(ncores)], axis=0)
    mk = np.stack([R[c]["mk"].reshape(DEPTH, NM, NH, 128) for c in range(ncores)], axis=1)
    mv = np.stack([R[c]["mv"].reshape(DEPTH, NM, NH, 128) for c in range(ncores)], axis=1)
    cap = np.stack([R[c]["cap"] for c in range(ncores)], axis=1)
    cbp = np.stack([R[c]["cbp"] for c in range(ncores)], axis=1)
    cas = np.concatenate([R[c]["cas"].reshape(NA, SB, WA - 1, MIXW) for c in range(ncores)], axis=1)
    cbs = np.concatenate([R[c]["cbs"].reshape(NB, SB, WB - 1, MIXW) for c in range(ncores)], axis=1)
    out = (y_p, y_s, mk, mv, cap, cbp, cas, cbs)
    return tuple(np.ascontiguousarray(o, dtype=np.float32) for o in out)
```
